# Optimizing a Trainium2 kernel written in Bass

```python
import jax, jax.numpy as jnp
from jax import lax
import numpy as np

D_MODEL = 1024
BATCH = 8
SEQ = 8192
DEPTH = 1

CTX_LEN = 256
GRID_W = 64
LRU_WIDTH = 512
LRU_HEADS = 8
LRU_HEAD_DIM = LRU_WIDTH // LRU_HEADS
LRU_CONV_W = 4
LRU_CONV_LEFT = 2
LRU_C = 8.0
MLA_HEADS = 8
QK_NOPE_DIM = 64
QK_ROPE_DIM = 32
QK_HEAD_DIM = QK_NOPE_DIM + QK_ROPE_DIM
V_HEAD_DIM = 64
Q_LORA_RANK = 256
KV_LORA_RANK = 128
MLA_WIDTH = MLA_HEADS * V_HEAD_DIM
MLA_SCALE = QK_HEAD_DIM ** -0.5
ROPE_PAIRS_PER_AXIS = QK_ROPE_DIM // 4
ROPE_BASE = 10000.0
Q_BLOCK = 128
MIX_WIDTH = LRU_WIDTH + MLA_WIDTH
OFF_GATE = LRU_WIDTH
OFF_CQ = 2 * LRU_WIDTH
OFF_CKV = OFF_CQ + Q_LORA_RANK
OFF_KR = OFF_CKV + KV_LORA_RANK
IN_PROJ_WIDTH = OFF_KR + QK_ROPE_DIM
D_FF = 2816
FFN_CONV_W = 3
FFN_CONV_LEFT = 1
N_MOD = 6
NORM_EPS = 1e-6

kernel_name = "hybrid_rglru_mla_convffn_dit_layer"


def rms_norm(x, g):
    xf = x.astype(jnp.float32)
    y = xf * lax.rsqrt(jnp.mean(xf * xf, axis=-1, keepdims=True) + NORM_EPS)
    return (y * g.astype(jnp.float32)).astype(x.dtype)


def modulate(h, shift, scale):
    return h * (1 + scale) + shift


def dwconv(x, w, b, left):
    k_width = w.shape[0]
    n = x.shape[1]
    xp = jnp.pad(x, ((0, 0), (left, k_width - 1 - left), (0, 0)))
    out = b
    for k in range(k_width):
        out = out + xp[:, k:k + n] * w[k]
    return out


def axial_rope_tables(n_tokens, dtype):
    rows = n_tokens // GRID_W
    row = jnp.repeat(jnp.arange(rows, dtype=jnp.float32), GRID_W)
    col = jnp.tile(jnp.arange(GRID_W, dtype=jnp.float32), rows)
    inv_freq = ROPE_BASE ** (-jnp.arange(ROPE_PAIRS_PER_AXIS, dtype=jnp.float32) / ROPE_PAIRS_PER_AXIS)
    ang = jnp.concatenate([row[:, None] * inv_freq, col[:, None] * inv_freq], axis=-1)
    return jnp.cos(ang).astype(dtype), jnp.sin(ang).astype(dtype)


def axial_rope(x, cos, sin):
    p = ROPE_PAIRS_PER_AXIS
    xr1, xr2, xc1, xc2 = jnp.split(x, 4, axis=-1)
    cr, cc = cos[..., :p], cos[..., p:]
    sr, sc = sin[..., :p], sin[..., p:]
    return jnp.concatenate([xr1 * cr - xr2 * sr, xr2 * cr + xr1 * sr,
                            xc1 * cc - xc2 * sc, xc2 * cc + xc1 * sc], axis=-1)


def split_in_proj(p):
    return (p[..., :OFF_GATE], p[..., OFF_GATE:OFF_CQ], p[..., OFF_CQ:OFF_CKV],
            p[..., OFF_CKV:OFF_KR], p[..., OFF_KR:IN_PROJ_WIDTH])


def rglru_coeffs(xc, w_a, b_a, w_x, b_x, lam):
    bsz, n, _ = xc.shape
    xh = xc.reshape(bsz, n, LRU_HEADS, LRU_HEAD_DIM)
    r = jax.nn.sigmoid(jnp.einsum('blhi,hij->blhj', xh, w_a).reshape(bsz, n, LRU_WIDTH) + b_a)
    i = jax.nn.sigmoid(jnp.einsum('blhi,hij->blhj', xh, w_x).reshape(bsz, n, LRU_WIDTH) + b_x)
    log_a = -LRU_C * r.astype(jnp.float32) * jax.nn.softplus(-lam.astype(jnp.float32))
    a = jnp.exp(log_a)
    u = jnp.sqrt(-jnp.expm1(2 * log_a)) * (i * xc).astype(jnp.float32)
    return a, u


def linear_scan(a, u, h0, reverse):
    if reverse:
        a, u = jnp.flip(a, 1), jnp.flip(u, 1)
    combine = lambda l, r: (l[0] * r[0], r[0] * l[1] + r[1])
    a_cum, u_cum = lax.associative_scan(combine, (a, u), axis=1)
    h = u_cum if h0 is None else a_cum * h0[:, None, :] + u_cum
    return jnp.flip(h, 1) if reverse else h


def mla_keys_values(ckv, kr, g_kv, w_ukv, cos, sin):
    bsz, n, _ = ckv.shape
    kv = (rms_norm(ckv, g_kv) @ w_ukv).reshape(bsz, n, MLA_HEADS, QK_NOPE_DIM + V_HEAD_DIM)
    k_nope, v = kv[..., :QK_NOPE_DIM], kv[..., QK_NOPE_DIM:]
    if cos is not None:
        kr = axial_rope(kr, cos, sin)
    k_rope = jnp.broadcast_to(kr[:, :, None, :], (bsz, n, MLA_HEADS, QK_ROPE_DIM))
    return jnp.concatenate([k_nope, k_rope], axis=-1), v


def mla_queries(cq, g_q, w_uq, cos, sin):
    bsz, n, _ = cq.shape
    q = (rms_norm(cq, g_q) @ w_uq).reshape(bsz, n, MLA_HEADS, QK_HEAD_DIM)
    q_nope, q_rope = q[..., :QK_NOPE_DIM], q[..., QK_NOPE_DIM:]
    if cos is not None:
        q_rope = axial_rope(q_rope, cos, sin)
    return jnp.concatenate([q_nope, q_rope], axis=-1)


def softmax_attention(q, k, v):
    s = jnp.einsum('bqhd,bkhd->bhqk', q, k).astype(jnp.float32) * MLA_SCALE
    p = jax.nn.softmax(s, axis=-1).astype(v.dtype)
    return jnp.einsum('bhqk,bkhd->bqhd', p, v)


def token_mixers(h_lat, h_ctx, cos, sin, w_in, lru_conv_w, lru_conv_b, lru_w_a, lru_b_a,
                 lru_w_x, lru_b_x, lru_lambda, mla_g_q, mla_w_uq, mla_g_kv, mla_w_ukv, w_out,
                 with_ctx_out):
    bsz, n_lat, _ = h_lat.shape
    n_ctx = h_ctx.shape[1]
    xr_l, gr_l, cq_l, ckv_l, kr_l = split_in_proj(h_lat @ w_in)
    xr_c, gr_c, cq_c, ckv_c, kr_c = split_in_proj(h_ctx @ w_in)

    xcv_l = dwconv(xr_l, lru_conv_w, lru_conv_b, LRU_CONV_LEFT)
    xcv_c = dwconv(xr_c, lru_conv_w, lru_conv_b, LRU_CONV_LEFT)
    lat_states, ctx_states = [], []
    for d, reverse in enumerate((False, True)):
        params_d = (lru_w_a[d], lru_b_a[d], lru_w_x[d], lru_b_x[d], lru_lambda[d])
        a_c, u_c = rglru_coeffs(xcv_c, *params_d)
        h_c = linear_scan(a_c, u_c, None, reverse)
        h0 = h_c[:, 0] if reverse else h_c[:, -1]
        a_l, u_l = rglru_coeffs(xcv_l, *params_d)
        lat_states.append(linear_scan(a_l, u_l, h0, reverse))
        ctx_states.append(h_c)
    y_lru_l = (lat_states[0] + lat_states[1]).astype(h_lat.dtype) * jax.nn.gelu(gr_l)

    k_c, v_c = mla_keys_values(ckv_c, kr_c, mla_g_kv, mla_w_ukv, None, None)
    k_l, v_l = mla_keys_values(ckv_l, kr_l, mla_g_kv, mla_w_ukv, cos, sin)
    q_l = mla_queries(cq_l, mla_g_q, mla_w_uq, cos[:, None], sin[:, None])
    k_all = jnp.concatenate([k_c, k_l], axis=1)
    v_all = jnp.concatenate([v_c, v_l], axis=1)
    n_blocks = n_lat // Q_BLOCK
    q_blocks = q_l.reshape(bsz, n_blocks, Q_BLOCK, MLA_HEADS, QK_HEAD_DIM).swapaxes(0, 1)
    o_blocks = lax.map(lambda qb: softmax_attention(qb, k_all, v_all), q_blocks)
    y_mla_l = o_blocks.swapaxes(0, 1).reshape(bsz, n_lat, MLA_WIDTH)
    y_lat = jnp.concatenate([y_lru_l, y_mla_l], axis=-1) @ w_out
    if not with_ctx_out:
        return y_lat, None

    y_lru_c = (ctx_states[0] + ctx_states[1]).astype(h_ctx.dtype) * jax.nn.gelu(gr_c)
    q_c = mla_queries(cq_c, mla_g_q, mla_w_uq, None, None)
    y_mla_c = softmax_attention(q_c, k_c, v_c).reshape(bsz, n_ctx, MLA_WIDTH)
    y_ctx = jnp.concatenate([y_lru_c, y_mla_c], axis=-1) @ w_out
    return y_lat, y_ctx


def conv_ffn(h, w_up, conv_w, conv_b, w_down):
    up = dwconv(h @ w_up, conv_w, conv_b, FFN_CONV_LEFT)
    u, g = up[..., :D_FF], up[..., D_FF:]
    return (jax.nn.silu(g) * u) @ w_down


def setup_inputs(seed: int = 0) -> dict:
    key = jax.random.key(seed)
    ks = jax.random.split(key, 32)
    L = DEPTH
    f32 = jnp.float32

    def nrm(k, shape, scale):
        return jax.random.normal(k, shape, f32) * scale

    def gain(k, shape):
        return 1.0 + 0.1 * jax.random.normal(k, shape, f32)

    a8 = jax.random.uniform(ks[17], (L, 2, LRU_WIDTH), f32, minval=0.9, maxval=0.999)
    a_base = a8 ** (1.0 / LRU_C)
    lam = jnp.log(a_base) - jnp.log1p(-a_base)
    return {
        "x": nrm(ks[0], (BATCH, SEQ, D_MODEL), 1.0),
        "c": nrm(ks[1], (BATCH, D_MODEL), 1.0),
        "ctx": nrm(ks[2], (BATCH, CTX_LEN, D_MODEL), 1.0),
        "c_ctx": nrm(ks[3], (D_MODEL,), 1.0),
        "w_mod": nrm(ks[4], (L, D_MODEL, N_MOD * D_MODEL), D_MODEL ** -0.5),
        "b_mod": nrm(ks[5], (L, N_MOD * D_MODEL), 0.02),
        "g_pre_mix": gain(ks[6], (L, D_MODEL)),
        "g_post_mix": gain(ks[7], (L, D_MODEL)),
        "g_pre_ffn": gain(ks[8], (L, D_MODEL)),
        "g_post_ffn": gain(ks[9], (L, D_MODEL)),
        "w_in": nrm(ks[10], (L, D_MODEL, IN_PROJ_WIDTH), D_MODEL ** -0.5),
        "lru_conv_w": nrm(ks[11], (L, LRU_CONV_W, LRU_WIDTH), LRU_CONV_W ** -0.5),
        "lru_conv_b": nrm(ks[12], (L, LRU_WIDTH), 0.02),
        "lru_w_a": nrm(ks[13], (L, 2, LRU_HEADS, LRU_HEAD_DIM, LRU_HEAD_DIM), LRU_HEAD_DIM ** -0.5),
        "lru_b_a": nrm(ks[14], (L, 2, LRU_WIDTH), 0.1),
        "lru_w_x": nrm(ks[15], (L, 2, LRU_HEADS, LRU_HEAD_DIM, LRU_HEAD_DIM), LRU_HEAD_DIM ** -0.5),
        "lru_b_x": nrm(ks[16], (L, 2, LRU_WIDTH), 0.1),
        "lru_lambda": lam,
        "mla_g_q": gain(ks[18], (L, Q_LORA_RANK)),
        "mla_w_uq": nrm(ks[19], (L, Q_LORA_RANK, MLA_HEADS * QK_HEAD_DIM), Q_LORA_RANK ** -0.5),
        "mla_g_kv": gain(ks[20], (L, KV_LORA_RANK)),
        "mla_w_ukv": nrm(ks[21], (L, KV_LORA_RANK, MLA_HEADS * (QK_NOPE_DIM + V_HEAD_DIM)), KV_LORA_RANK ** -0.5),
        "w_out": nrm(ks[22], (L, MIX_WIDTH, D_MODEL), MIX_WIDTH ** -0.5),
        "ffn_w_up": nrm(ks[23], (L, D_MODEL, 2 * D_FF), D_MODEL ** -0.5),
        "ffn_conv_w": nrm(ks[24], (L, FFN_CONV_W, 2 * D_FF), FFN_CONV_W ** -0.5),
        "ffn_conv_b": nrm(ks[25], (L, 2 * D_FF), 0.02),
        "ffn_w_down": nrm(ks[26], (L, D_FF, D_MODEL), D_FF ** -0.5),
    }


def reference(x, c, ctx, c_ctx, w_mod, b_mod, g_pre_mix, g_post_mix, g_pre_ffn, g_post_ffn,
              w_in, lru_conv_w, lru_conv_b, lru_w_a, lru_b_a, lru_w_x, lru_b_x, lru_lambda,
              mla_g_q, mla_w_uq, mla_g_kv, mla_w_ukv, w_out, ffn_w_up, ffn_conv_w, ffn_conv_b,
              ffn_w_down):
    n_lat = x.shape[1]
    cos, sin = axial_rope_tables(n_lat, x.dtype)
    xc = ctx
    for l in range(DEPTH):
        last = l == DEPTH - 1
        mod_l = jax.nn.silu(c) @ w_mod[l] + b_mod[l]
        mod_c = jax.nn.silu(c_ctx) @ w_mod[l] + b_mod[l]
        sh1, sc1, gt1, sh2, sc2, gt2 = jnp.split(mod_l[:, None, :], N_MOD, axis=-1)
        csh1, csc1, cgt1, csh2, csc2, cgt2 = jnp.split(mod_c, N_MOD, axis=-1)

        h_l = modulate(rms_norm(x, g_pre_mix[l]), sh1, sc1)
        h_c = modulate(rms_norm(xc, g_pre_mix[l]), csh1, csc1)
        y_l, y_c = token_mixers(h_l, h_c, cos, sin, w_in[l], lru_conv_w[l], lru_conv_b[l],
                                lru_w_a[l], lru_b_a[l], lru_w_x[l], lru_b_x[l], lru_lambda[l],
                                mla_g_q[l], mla_w_uq[l], mla_g_kv[l], mla_w_ukv[l], w_out[l],
                                not last)
        x = x + gt1 * rms_norm(y_l, g_post_mix[l])
        f_l = conv_ffn(modulate(rms_norm(x, g_pre_ffn[l]), sh2, sc2),
                       ffn_w_up[l], ffn_conv_w[l], ffn_conv_b[l], ffn_w_down[l])
        x = x + gt2 * rms_norm(f_l, g_post_ffn[l])

        if not last:
            xc = xc + cgt1 * rms_norm(y_c, g_post_mix[l])
            f_c = conv_ffn(modulate(rms_norm(xc, g_pre_ffn[l]), csh2, csc2),
                           ffn_w_up[l], ffn_conv_w[l], ffn_conv_b[l], ffn_w_down[l])
            xc = xc + cgt2 * rms_norm(f_c, g_post_ffn[l])
    return x
```

```python
import numpy as np
from contextlib import ExitStack
import ml_dtypes
import concourse.bass as bass
import concourse.mybir as mybir
from concourse.bass_utils import run_bass_kernel_spmd
from concourse.alu_op_type import AluOpType as ALU

F32 = mybir.dt.float32
BF16 = mybir.dt.bfloat16
AF = mybir.ActivationFunctionType
AX = mybir.AxisListType

D = 1024
KC = 8
CT = 256
LRU_W = 512
NH = 8
DFF = 2816
FC = DFF // 128
EPS = 1e-6
MLA_SCALE = 96 ** -0.5
GELU_C = 0.7978845608028654
OFF_GATE, OFF_CQ, OFF_CKV, OFF_KR = 512, 1024, 1280, 1408
ROLL = 4096
KRING = 16


class Sync:
    def __init__(self, nc, es):
        self.nc, self.es = nc, es
        self.csem = {e: [] for e in ("pe", "act", "dve", "pool")}
        self.cnt = {e: 0 for e in ("pe", "act", "dve", "pool")}
        self.ring = {q: [es.enter_context(nc.semaphore(f"dq_{q}{i}")) for i in range(KRING)]
                     for q in ("sp", "act", "pool")}
        self.dcnt = {q: 0 for q in ("sp", "act", "pool")}
        self.known = {f: {e: 0 for e in ("pe", "act", "dve", "pool")} for f in ("pe", "act", "dve", "pool", "sp")}
        self.knownd = {f: {} for f in ("pe", "act", "dve", "pool", "sp")}

    def sem_for(self, eng, val):
        idx = (val - 1) // ROLL
        while len(self.csem[eng]) <= idx:
            self.csem[eng].append(self.es.enter_context(self.nc.semaphore(f"c_{eng}{len(self.csem[eng])}")))
        return self.csem[eng][idx], (val - 1) % ROLL + 1


class Ins:
    __slots__ = ("eng", "fn", "r", "w", "w0", "dma", "deps", "inc", "val", "dsem", "dval", "waits", "raw", "dur", "lat")

    def __init__(self, eng, fn, r, w, dma, dur=100.0, lat=0.0, w0=()):
        self.eng, self.fn, self.r, self.w, self.dma = eng, fn, r, w, dma
        self.w0 = w0
        self.dur, self.lat = dur, lat
        self.deps = {}
        self.inc = False
        self.val = 0
        self.waits = []


class Prog:
    def __init__(self, sy):
        self.sy = sy
        self.I = []

    limit = None

    def add(self, eng, fn, r=(), w=(), dma=False, dur=100.0, lat=0.0):
        if Prog.limit is not None and len(self.I) >= Prog.limit:
            return
        r, w0 = tuple(r), tuple(w)
        w = w0 + tuple(t for t in r if t.startswith("ps") and t not in w0)
        self.I.append(Ins(eng, fn, r, w, dma, dur, lat, w0))

    @staticmethod
    def _n(ap):
        n = 1
        for d in ap.shape[1:]:
            n *= d
        return n

    def mm(self, out, lhsT, rhs, start, stop, r, w):
        n = self._n(rhs)
        dur = max(n, 64) / 2.0 * (4.0 if rhs.dtype == F32 else 1.0) + 12.0
        self.add("pe", lambda e: e.matmul(out, lhsT=lhsT, rhs=rhs, start=start, stop=stop), r, w, dur=dur)

    def tr(self, out, in_, ident, r, w):
        self.add("pe", lambda e: e.transpose(out=out, in_=in_, identity=ident), r, w, dur=80.0)

    act_ord = False

    def act(self, out, in_, func, r, w, scale=1.0, bias=0.0, accum=None, eng="act"):
        dur = (self._n(in_) + 230) / 1.2 + (90.0 if accum is not None else 0.0)
        if self.act_ord:
            w = list(w) + ["ord:act"]
        if accum is None:
            self.add(eng, lambda e: e.activation(out=out, in_=in_, func=func, bias=bias, scale=scale), r, w, dur=dur)
        else:
            self.add(eng, lambda e: e.activation(out=out, in_=in_, func=func, bias=bias, scale=scale,
                                                 accum_out=accum), r, w, dur=dur)

    def _vd(self, n, eng, mult=1.0):
        return (n * mult / 0.96 + 70.0) * (2.0 if eng == "pool" else 1.0)

    def ts(self, out, in0, s1, s2, op0, op1, r, w, eng="dve"):
        dur = self._vd(self._n(in0), eng, 0.7)
        if op1 is None:
            self.add(eng, lambda e: e.tensor_scalar(out=out, in0=in0, scalar1=s1, scalar2=None, op0=op0), r, w, dur=dur)
        else:
            self.add(eng, lambda e: e.tensor_scalar(out=out, in0=in0, scalar1=s1, scalar2=s2, op0=op0, op1=op1), r, w, dur=dur)

    def tt(self, out, in0, in1, op, r, w, eng="dve"):
        self.add(eng, lambda e: e.tensor_tensor(out=out, in0=in0, in1=in1, op=op), r, w, dur=self._vd(self._n(in0), eng))

    def stt(self, out, in0, scalar, in1, op0, op1, r, w):
        self.add("dve", lambda e: e.scalar_tensor_tensor(out=out, in0=in0, scalar=scalar, in1=in1, op0=op0, op1=op1), r, w,
                 dur=self._vd(self._n(in0), "dve"))

    def cp(self, out, in_, r, w, eng="dve"):
        self.add(eng, lambda e: e.tensor_copy(out=out, in_=in_), r, w, dur=self._vd(self._n(in_), eng, 0.7))

    def recip(self, out, in_, r, w):
        self.add("dve", lambda e: e.reciprocal(out=out, in_=in_), r, w, dur=self._vd(self._n(in_), "dve", 8.0))

    def scan(self, out, a, u, initial, r, w):
        self.add("dve", lambda e: e.tensor_tensor_scan(out=out, data0=a, data1=u, initial=initial,
                                                       op0=ALU.mult, op1=ALU.add), r, w, dur=self._vd(self._n(a), "dve", 2.0))

    def memset(self, ap, val, w, eng="pool"):
        self.add(eng, lambda e: e.memset(ap, val), (), w, dur=self._vd(self._n(ap), eng, 0.5))

    def dma(self, out, in_, r, w, q="sp", slow=False):
        nbytes = self._n(out) * out.shape[0] * (4 if out.dtype == F32 else 2)
        lat = 2000.0 + nbytes / 60.0
        if slow:
            self.add(q, lambda e: e.dma_start(out=out, in_=in_, allow_slow_non_contiguous=True), r, w, dma=True, dur=60.0, lat=lat)
        else:
            self.add(q, lambda e: e.dma_start(out=out, in_=in_), r, w, dma=True, dur=60.0, lat=lat)

    schedule = True

    def _schedule(self, I):
        n = len(I)
        succ = [[] for _ in range(n)]
        npred = [0] * n
        for i, ins in enumerate(I):
            npred[i] = len(ins.deps)
            for p in ins.deps:
                succ[p].append(i)
        engs = ("pe", "act", "dve", "pool", "sp")
        te = {e: 0.0 for e in engs}
        fin = [0.0] * n
        rdy = [0.0] * n
        ready = {e: [] for e in engs}
        import heapq
        for i in range(n):
            if npred[i] == 0:
                heapq.heappush(ready[I[i].eng], (0.0, i))
        order = []
        WIN = 6
        while len(order) < n:
            best = None
            for e in engs:
                h = ready[e]
                if not h:
                    continue
                cand = heapq.nsmallest(WIN, h)
                for rt, i in cand:
                    st_ = max(te[e], rt)
                    key = (st_, i)
                    if best is None or key < best[0]:
                        best = (key, e, (rt, i))
            (st_, i), e, item = best
            ready[e].remove(item)
            heapq.heapify(ready[e])
            ins = I[i]
            te[e] = st_ + ins.dur
            fin[i] = st_ + ins.dur + ins.lat
            order.append(i)
            for sidx in succ[i]:
                npred[sidx] -= 1
                rdy[sidx] = max(rdy[sidx], fin[i] + (0.0 if I[sidx].eng == e and not ins.dma else 150.0))
                if npred[sidx] == 0:
                    heapq.heappush(ready[I[sidx].eng], (rdy[sidx], sidx))
        pos = {old: new for new, old in enumerate(order)}
        out = []
        for old in order:
            ins = I[old]
            ins.deps = {pos[p]: raw for p, raw in ins.deps.items()}
            out.append(ins)
        self.est = max(fin) if n else 0.0
        return out

    def finalize(self, block):
        sy = self.sy
        I = self.I
        last_w, readers = {}, {}
        for idx, ins in enumerate(I):
            deps = ins.deps
            for t in ins.r:
                if t in last_w:
                    deps[last_w[t]] = True
            for t in ins.w:
                if t in last_w:
                    deps.setdefault(last_w[t], False)
                for ri in readers.get(t, ()):
                    deps.setdefault(ri, False)
            deps.pop(idx, None)
            for t in ins.r:
                readers.setdefault(t, []).append(idx)
            for t in ins.w:
                last_w[t] = idx
                readers[t] = []
        if Prog.schedule:
            I = self._schedule(I)
            self.I = I
        need = []
        for idx, ins in enumerate(I):
            lst = []
            for pi, raw in ins.deps.items():
                p = I[pi]
                if p.dma:
                    lst.append(pi)
                    continue
                if p.eng == ins.eng and not ins.dma:
                    if ins.eng == "pe":
                        continue
                    if not (any((t in p.r or t in p.w0) and not t.startswith("ord:") for t in ins.w0)
                            or any(t in p.w0 and not t.startswith("ord:") for t in ins.r)):
                        continue
                lst.append(pi)
                p.inc = True
            need.append(lst)
        lastc = {}
        for idx, ins in enumerate(I):
            if not ins.dma and ins.fn is not None:
                lastc[ins.eng] = idx
        for e, idx in lastc.items():
            I[idx].inc = True
        for idx, ins in enumerate(I):
            f = ins.eng
            waits = []
            best = {}
            for pi in need[idx]:
                p = I[pi]
                if p.dma:
                    key = (p.eng, id(p.dsem))
                    if sy.knownd[f].get(key, 0) < p.dval:
                        sy.knownd[f][key] = p.dval
                        waits.append((p.dsem, p.dval))
                else:
                    best[p.eng] = max(best.get(p.eng, 0), p.val)
            for e, v in best.items():
                if sy.known[f][e] < v:
                    sy.known[f][e] = v
                    waits.append(sy.sem_for(e, v))
            if ins.dma:
                q = ins.eng
                n = sy.dcnt[q]
                sy.dcnt[q] += 1
                ins.dsem = sy.ring[q][n % KRING]
                ins.dval = 16 * (n // KRING + 1)
                if n >= KRING:
                    key = (q, id(ins.dsem))
                    if sy.knownd[f].get(key, 0) < ins.dval - 16:
                        sy.knownd[f][key] = ins.dval - 16
                        waits.append((ins.dsem, ins.dval - 16))
            elif ins.inc:
                sy.cnt[f] += 1
                ins.val = sy.cnt[f]
            ins.waits = waits
        bar = {}
        for f in ("pe", "act", "dve", "pool", "sp"):
            waits = []
            for e, idx in lastc.items():
                if e != f and sy.known[f][e] < I[idx].val:
                    sy.known[f][e] = I[idx].val
                    waits.append(sy.sem_for(e, I[idx].val))
            for q in ("sp", "act", "pool"):
                n = sy.dcnt[q]
                for s in range(KRING):
                    uses = (n - s + KRING - 1) // KRING if n > s else 0
                    if uses > 0:
                        key = (q, id(sy.ring[q][s]))
                        if sy.knownd[f].get(key, 0) < 16 * uses:
                            sy.knownd[f][key] = 16 * uses
                            waits.append((sy.ring[q][s], 16 * uses))
            bar[f] = waits
        streams = {f: [ins for ins in I if ins.eng == f] for f in ("pe", "act", "dve", "pool", "sp")}

        def run(e, f):
            for ins in streams[f]:
                for s, v in ins.waits:
                    e.wait_ge(s, v)
                if ins.fn is None:
                    continue
                h = ins.fn(e)
                if ins.dma:
                    h.then_inc(ins.dsem, 16)
                elif ins.inc:
                    s, _ = sy.sem_for(f, ins.val)
                    h.then_inc(s, 1)
            for s, v in bar[f]:
                e.wait_ge(s, v)

        block.tensor(lambda e: run(e, "pe"))
        block.scalar(lambda e: run(e, "act"))
        block.vector(lambda e: run(e, "dve"))
        block.gpsimd(lambda e: run(e, "pool"))
        block.sync(lambda e: run(e, "sp"))


def _pp(v, nchunk):
    return np.ascontiguousarray(np.asarray(v, np.float32).reshape(nchunk, 128).T)


def _rope_tables(S):
    t = np.arange(S)
    row = (t // 64).astype(np.float32)
    col = (t % 64).astype(np.float32)
    inv = (np.float32(10000.0) ** (-np.arange(8, dtype=np.float32) / np.float32(8))).astype(np.float32)
    ar = (row[None, :] * inv[:, None]).astype(np.float32)
    ac = (col[None, :] * inv[:, None]).astype(np.float32)
    cr, sr, cc, sc = np.cos(ar), np.sin(ar), np.cos(ac), np.sin(ac)
    C = np.concatenate([cr, cr, cc, cc], 0).astype(np.float32)
    Sg = np.concatenate([-sr, sr, -sc, sc], 0).astype(np.float32)
    return np.ascontiguousarray(np.tile(C, (4, 1))), np.ascontiguousarray(np.tile(Sg, (4, 1)))


_PERM32 = np.concatenate([np.arange(8, 16), np.arange(0, 8), np.arange(24, 32), np.arange(16, 24)])


def host_prep(inp, S):
    f32 = np.float32
    g = lambda k: np.asarray(inp[k], f32)
    B = inp["x"].shape[0]
    w_in = g("w_in")[0]
    w_in_ext = np.ascontiguousarray(np.concatenate([w_in, w_in[:, OFF_KR + _PERM32]], axis=1))
    w_uq = g("mla_w_uq")[0].reshape(256, NH, 96)
    w_uq_n = np.ascontiguousarray(w_uq[:, :, :64].reshape(256, 512))
    w_uq_r = np.ascontiguousarray(w_uq[:, :, 64:].reshape(256, 256))
    w_uq_rb = np.ascontiguousarray(w_uq[:, :, 64 + _PERM32].reshape(256, 256))
    w_ukv = g("mla_w_ukv")[0].reshape(128, NH, 128)
    w_ukv_k = np.ascontiguousarray(w_ukv[:, :, :64].reshape(128, 512))
    w_ukv_v = np.ascontiguousarray(w_ukv[:, :, 64:].reshape(128, 512))
    bd = np.zeros((16, 128, 128), f32)
    for d in range(2):
        for gi, nm in enumerate(("lru_w_a", "lru_w_x")):
            w = g(nm)[0, d]
            for c in range(4):
                i = (d * 2 + gi) * 4 + c
                bd[i, :64, :64] = w[2 * c]
                bd[i, 64:, 64:] = w[2 * c + 1]
    lvec = np.zeros((128, 24), f32)
    for vi, nm in enumerate(("lru_b_a", "lru_b_x", "lru_lambda")):
        for d in range(2):
            lvec[:, vi * 8 + d * 4: vi * 8 + d * 4 + 4] = _pp(g(nm)[0, d], 4)
    lcw = np.concatenate([_pp(g("lru_conv_w")[0, k], 4) for k in range(4)], axis=1)
    fcw = np.concatenate([_pp(g("ffn_conv_w")[0, k], 44) for k in range(3)], axis=1)
    ropeC, ropeS = _rope_tables(S)
    common = {
        "w_mod": g("w_mod")[0], "bmod_pp": _pp(g("b_mod")[0], 48), "b_mod": g("b_mod"),
        "gvec_pp": np.concatenate([_pp(g("g_pre_mix")[0], 8), _pp(g("g_pre_ffn")[0], 8)], axis=1),
        "g_post": np.ascontiguousarray(np.stack([g("g_post_mix")[0], g("g_post_ffn")[0]])),
        "w_in_ext": w_in_ext, "lru_cw": np.ascontiguousarray(lcw), "lru_cb": _pp(g("lru_conv_b")[0], 4),
        "lru_bd": bd, "lru_vec": lvec,
        "gq": _pp(g("mla_g_q")[0], 2), "gkv": _pp(g("mla_g_kv")[0], 1),
        "w_uq_n": w_uq_n, "w_uq_r": w_uq_r, "w_uq_rb": w_uq_rb, "w_ukv_k": w_ukv_k, "w_ukv_v": w_ukv_v,
        "w_out": g("w_out")[0], "w_up": g("ffn_w_up")[0], "w_dn": g("ffn_w_down")[0],
        "ffn_cw": np.ascontiguousarray(fcw), "ffn_cb": _pp(g("ffn_conv_b")[0], 44),
        "ropeC": ropeC, "ropeS": ropeS, "ident": np.eye(128).astype(ml_dtypes.bfloat16),
    }
    maps = []
    cc = _pp(g("c_ctx"), 8)
    for b in range(B):
        m = dict(common)
        m["x"] = np.ascontiguousarray(g("x")[b, :S])
        m["ctx"] = np.ascontiguousarray(g("ctx")[b])
        m["cvec"] = np.ascontiguousarray(np.concatenate([_pp(g("c")[b], 8), cc], axis=1))
        maps.append(m)
    return maps


def build(S, dbg=(), stop=None):
    nc = bass.Bass("TRN2", target_bir_lowering=False)
    NK = CT + S
    NKC = NK // 128
    NT = S // 512
    XO_C, XO_L = 2, 262
    LX = 264 + S

    def din(name, shape, dt=F32):
        return nc.dram_tensor(name, list(shape), dt, kind="ExternalInput").ap()

    def dscr(name, shape, dt):
        kind = "ExternalOutput" if name in dbg else "Internal"
        return nc.dram_tensor(name, list(shape), dt, kind=kind).ap()

    x_d = din("x", [S, D]); ctx_d = din("ctx", [CT, D])
    cvec_d = din("cvec", [128, 16]); wmod_d = din("w_mod", [D, 6 * D])
    bmodpp_d = din("bmod_pp", [128, 48]); bmod_d = din("b_mod", [1, 6 * D])
    gvec_d = din("gvec_pp", [128, 16]); gpost_d = din("g_post", [2, D])
    win_d = din("w_in_ext", [D, 1472])
    lcw_d = din("lru_cw", [128, 16]); lcb_d = din("lru_cb", [128, 4])
    lbd_d = din("lru_bd", [16, 128, 128]); lvec_d = din("lru_vec", [128, 24])
    gq_d = din("gq", [128, 2]); gkv_d = din("gkv", [128, 1])
    wuqn_d = din("w_uq_n", [256, 512]); wuqr_d = din("w_uq_r", [256, 256]); wuqb_d = din("w_uq_rb", [256, 256])
    wukvk_d = din("w_ukv_k", [128, 512]); wukvv_d = din("w_ukv_v", [128, 512])
    wout_d = din("w_out", [D, D]); wup_d = din("w_up", [D, 2 * DFF]); wdn_d = din("w_dn", [DFF, D])
    fcw_d = din("ffn_cw", [128, 3 * 44]); fcb_d = din("ffn_cb", [128, 44])
    ropeC_d = din("ropeC", [128, S]); ropeS_d = din("ropeS", [128, S])
    ident_d = din("ident", [128, 128], BF16)
    out_d = nc.dram_tensor("out", [S, D], F32, kind="ExternalOutput").ap()

    qT_d = dscr("qT", [NH, 96, S], BF16)
    kT_d = dscr("kT", [NH, 64, NK], BF16)
    krT_d = dscr("krT", [32, NK], BF16)
    vimg_d = dscr("vimg", [NH, 128, NKC, 64], BF16)
    gT_d = dscr("gT", [4, 128, S], BF16)
    yT_d = dscr("yT", [8, 128, S], BF16)
    hf_d = dscr("hfs", [4, 128, S], BF16)
    x1_d = dscr("x1s", [S, D], F32)
    h2_d = dscr("h2s", [8, 128, S + 2], BF16)
    dbgmod_d = dscr("dbgmod", [128, 48 + 2048], F32) if "dbgmod" in dbg else None
    dbgxr_d = dscr("dbgxr", [4, 128, LX], BF16) if "dbgxr" in dbg else None

    with ExitStack() as es:
        sy = Sync(nc, es)

        def sb(st, name, shape, dt):
            return st.enter_context(nc.sbuf_tensor("s_" + name, list(shape), dt))

        def ps(st, name, shape, dt=F32):
            return st.enter_context(nc.psum_tensor("p_" + name, list(shape), dt))

        ident = sb(es, "ident", [128, 128], BF16)
        modv = sb(es, "modv", [128, 6, 8], F32)
        gpbc = sb(es, "gpbc", [128, 2, D], F32)
        onesb = sb(es, "onesb", [128, 128], BF16)
        onesf = sb(es, "onesf", [128, 128], F32)
        epsc = sb(es, "epsc", [128, 1], F32)

        XO_C, XO_L = 2, 262
        stX = ExitStack()
        xrT = sb(stX, "xrT", [128, 4, LX], BF16)
        stW = ExitStack()
        win = sb(stW, "win", [128, 8, 1472], BF16)
        wuq = sb(stW, "wuq", [128, 2, 1024], BF16)
        wukv = sb(stW, "wukv", [128, 1024], BF16)
        gqs = sb(stW, "gqs", [128, 2], F32)
        gkvs = sb(stW, "gkvs", [128, 1], F32)

        with ExitStack() as st, nc.Block() as block:
            P = Prog(sy)
            P.dma(win[:], win_d.rearrange("(k p) f -> p k f", p=128), (), ["win"], q="pool")
            P.dma(wuq[:, :, 0:512], wuqn_d.rearrange("(k p) f -> p k f", p=128), (), ["wuq"], q="pool")
            P.dma(wuq[:, :, 512:768], wuqr_d.rearrange("(k p) f -> p k f", p=128), (), ["wuq"], q="pool")
            P.dma(wuq[:, :, 768:1024], wuqb_d.rearrange("(k p) f -> p k f", p=128), (), ["wuq"], q="pool")
            P.dma(wukv[:, 0:512], wukvk_d, (), ["wukv"], q="pool")
            P.dma(wukv[:, 512:1024], wukvv_d, (), ["wukv"], q="pool")
            P.dma(gqs[:], gq_d, (), ["gqs"])
            P.dma(gkvs[:], gkv_d, (), ["gkvs"])
            cv = sb(st, "cv", [128, 16], F32)
            sv = sb(st, "sv", [128, 16], F32)
            th = sb(st, "th", [128, 16], F32)
            svb = sb(st, "svb", [128, 8, 128], F32)
            wm = sb(st, "wm", [128, 2, 8, 512], F32)
            bpp = sb(st, "bpp", [128, 48], F32)
            gv = sb(st, "gv", [128, 16], F32)
            modpp = sb(st, "modpp", [128, 4, 8, 2], F32)
            bbc = sb(st, "bbc", [128, 2, D], F32)
            gbc = sb(st, "gbc", [128, 2, D], F32)
            tmpb = sb(st, "tmpb", [128, 512], F32)
            pspp = ps(st, "pspp", [128, 4, 8, 2], F32)
            psbc = ps(st, "psbc", [128, 2, 512], F32)

            P.dma(ident[:], ident_d, (), ["ident"])
            P.dma(cv[:], cvec_d, (), ["cv"])
            P.dma(bpp[:], bmodpp_d, (), ["bpp"])
            P.dma(gv[:], gvec_d, (), ["gv"])
            P.dma(bbc[:, 0, :], bmod_d[0:1, 2 * D:3 * D].partition_broadcast(128), (), ["bbc0"])
            P.dma(bbc[:, 1, :], bmod_d[0:1, 5 * D:6 * D].partition_broadcast(128), (), ["bbc1"])
            P.dma(gbc[:, 0, :], gpost_d[0:1, :].partition_broadcast(128), (), ["gbc0"])
            P.dma(gbc[:, 1, :], gpost_d[1:2, :].partition_broadcast(128), (), ["gbc1"])
            P.memset(onesb[:], 1.0, ["onesb"])
            P.memset(onesf[:], 1.0, ["onesf"])
            P.memset(epsc[:], EPS, ["epsc"])
            P.act(th[:], cv[:], AF.Tanh, ["cv"], ["th"], scale=0.5)
            P.stt(sv[:], th[:], 1.0, cv[:], ALU.add, ALU.mult, ["th", "cv"], ["sv"])
            P.ts(sv[:], sv[:], 0.5, None, ALU.mult, None, ["sv"], ["sv"])
            for k in range(8):
                P.ts(svb[:, k, :], onesf[:], sv[:, k:k + 1], None, ALU.mult, None, ["onesf", "sv"], [f"svb{k}"])
            wmv = wmod_d.rearrange("(k p) f -> p k f", p=128)
            vec_of = {0: 0, 1: 1, 3: 2, 4: 3}
            for j in range(12):
                sl = j % 2
                P.dma(wm[:, sl, :, :], wmv[:, :, j * 512:(j + 1) * 512], (), [f"wm{sl}"], q="sp" if j % 2 == 0 else "act")
                v, half = j // 2, j % 2
                if v in vec_of:
                    vi = vec_of[v]
                    for fc in range(4):
                        ch = half * 4 + fc
                        for k in range(8):
                            P.mm(pspp[:, vi, ch, :], wm[:, sl, k, fc * 128:(fc + 1) * 128], sv[:, k:16:8],
                                 k == 0, k == 7, [f"wm{sl}", "sv"], ["pspp"])
                else:
                    vv = 0 if v == 2 else 1
                    for k in range(8):
                        P.mm(psbc[:, half, :], svb[:, k, :], wm[:, sl, k, :], k == 0, k == 7,
                             [f"wm{sl}", f"svb{k}"], [f"psbc{half}"])
                    P.tt(tmpb[:], psbc[:, half, :], bbc[:, vv, half * 512:(half + 1) * 512], ALU.add,
                         [f"psbc{half}", f"bbc{vv}"], ["tmpb"])
                    P.tt(gpbc[:, vv, half * 512:(half + 1) * 512], tmpb[:], gbc[:, vv, half * 512:(half + 1) * 512],
                         ALU.mult, ["tmpb", f"gbc{vv}"], [f"gpbc{vv}{half}"])
            for vi, vsrc in enumerate((0, 1, 3, 4)):
                for jj in range(2):
                    P.tt(modpp[:, vi, :, jj], pspp[:, vi, :, jj], bpp[:, vsrc * 8:(vsrc + 1) * 8], ALU.add,
                         ["pspp", "bpp"], [f"modpp{vi}{jj}"])
            P.stt(modv[:, 0, :], modpp[:, 1, :, 0], 1.0, gv[:, 0:8], ALU.add, ALU.mult, ["modpp10", "gv"], ["modv0"])
            P.cp(modv[:, 1, :], modpp[:, 0, :, 0], ["modpp00"], ["modv1"])
            P.stt(modv[:, 2, :], modpp[:, 1, :, 1], 1.0, gv[:, 0:8], ALU.add, ALU.mult, ["modpp11", "gv"], ["modv2"])
            P.cp(modv[:, 3, :], modpp[:, 0, :, 1], ["modpp01"], ["modv3"])
            P.stt(modv[:, 4, :], modpp[:, 3, :, 0], 1.0, gv[:, 8:16], ALU.add, ALU.mult, ["modpp30", "gv"], ["modv4"])
            P.cp(modv[:, 5, :], modpp[:, 2, :, 0], ["modpp20"], ["modv5"])
            if dbgmod_d is not None:
                P.dma(dbgmod_d[:, 0:48], modv[:].rearrange("p a b -> p (a b)"),
                      [f"modv{i}" for i in range(6)], ["dbgmod"])
                P.dma(dbgmod_d[:, 48:48 + 2048], gpbc[:].rearrange("p a b -> p (a b)"),
                      ["gpbc00", "gpbc01", "gpbc10", "gpbc11"], ["dbgmod2"])
            P.finalize(block)
        if stop == "P0":
            return nc

        tiles = [("c", 0, CT)] + [("l", i * 512, 512) for i in range(NT)]

        with ExitStack() as stAB:

            with ExitStack() as st, nc.Block() as block:
                P = Prog(sy)
                xt = sb(st, "xt", [128, 4, D], F32)
                xjunk = sb(st, "xjunk", [128, D], BF16)
                xh = sb(st, "xh", [128, 4, D], BF16)
                st8 = sb(st, "st8", [128, 3, 4], F32)
                hT = sb(st, "hT", [128, 8, 512], BF16)
                tabC2 = sb(st, "tabC", [128, 2, 512], F32)
                tabS2 = sb(st, "tabS", [128, 2, 512], F32)
                gx = sb(st, "gx", [128, 3, 512], F32)
                gst = sb(st, "gst", [128, 4, 512], BF16)
                sqq = sb(st, "sqq", [128, 3, 512], BF16)
                cqT = sb(st, "cqT", [128, 2, 512], BF16)
                ckvg = sb(st, "ckvg", [128, 512], F32)
                ckvn = sb(st, "ckvn", [128, 512], BF16)
                rq = sb(st, "rq", [128, 2, 512], F32)
                rkv = sb(st, "rkv", [128, 2, 512], F32)
                cqs = sb(st, "cqs", [128, 2, 512], F32)
                tr1 = sb(st, "tr1", [128, 2, 512], F32)
                qn_st = sb(st, "qn_st", [128, 4, 512], BF16)
                qr_st = sb(st, "qr_st", [128, 2, 512], BF16)
                kn_st = sb(st, "kn_st", [128, 4, 512], BF16)
                kr_st = sb(st, "kr_st", [32, 512], BF16)
                v_st = sb(st, "v_st", [128, 4, 512], BF16)
                psT = ps(st, "psT", [128, 2, 2, 512], BF16)
                psI = ps(st, "psI", [128, 3, 512], F32)
                psM = ps(st, "psM", [128, 3, 512], F32)
                cI = [0]
                cM = [0]

                def nI():
                    cI[0] += 1
                    return (cI[0] - 1) % 3

                def nM():
                    cM[0] += 1
                    return (cM[0] - 1) % 3

                for c in range(4):
                    P.memset(xrT[:, c, 0:2], 0.0, ["xrpad"])
                    P.memset(xrT[:, c, 258:262], 0.0, ["xrpad"])
                    P.memset(xrT[:, c, 262 + S:264 + S], 0.0, ["xrpad"])

                def load_x(ti):
                    kind, t0, T = tiles[ti]
                    src = ctx_d if kind == "c" else x_d
                    for j in range(T // 128):
                        P.dma(xt[:, j, :], src[t0 + j * 128:t0 + (j + 1) * 128, :], (), [f"xt{j}"],
                              q="sp" if j % 2 == 0 else "act")
                    if kind == "l":
                        P.dma(tabC2[:, ti % 2, :], ropeC_d[:, t0:t0 + T], (), [f"tabC{ti % 2}"])
                        P.dma(tabS2[:, ti % 2, :], ropeS_d[:, t0:t0 + T], (), [f"tabS{ti % 2}"])

                load_x(0)
                for ti, (kind, t0, T) in enumerate(tiles):
                    lat = kind == "l"
                    ns = T // 128
                    g0 = t0 + (CT if lat else 0)
                    xo = (XO_L if lat else XO_C) + t0
                    mi = 0 if lat else 2
                    tabC, tabS = tabC2[:, ti % 2, :], tabS2[:, ti % 2, :]
                    tC, tS = f"tabC{ti % 2}", f"tabS{ti % 2}"
                    for j in range(ns):
                        P.act(xjunk[:], xt[:, j, :], AF.Square, [f"xt{j}"], ["xjunk", f"ss{j}"], accum=st8[:, 0, j:j + 1])
                    P.ts(st8[:, 1, 0:ns], st8[:, 0, 0:ns], 1.0 / D, EPS, ALU.mult, ALU.add,
                         [f"ss{j}" for j in range(ns)], ["ms"])
                    P.act(st8[:, 1, 0:ns], st8[:, 1, 0:ns], AF.Sqrt, ["ms"], ["ms"])
                    P.recip(st8[:, 2, 0:ns], st8[:, 1, 0:ns], ["ms"], ["rstd"])
                    for j in range(ns):
                        P.ts(xh[:, j, :], xt[:, j, :], st8[:, 2, j:j + 1], None, ALU.mult, None,
                             [f"xt{j}", "rstd"], [f"xh{j}"])
                    for cp in range(4):
                        sl = cp % 2
                        for j in range(ns):
                            for cc in range(2):
                                c = cp * 2 + cc
                                P.tr(psT[:, sl, cc, j * 128:(j + 1) * 128], xh[:, j, c * 128:(c + 1) * 128], ident[:],
                                     [f"xh{j}", "ident"], [f"psT{sl}"])
                        for cc in range(2):
                            c = cp * 2 + cc
                            P.act(hT[:, c, 0:T], psT[:, sl, cc, 0:T], AF.Identity, [f"psT{sl}", f"modv{mi}", f"modv{mi + 1}"],
                                  [f"hT{c}"], scale=modv[:, mi, c:c + 1], bias=modv[:, mi + 1, c:c + 1])
                    if ti + 1 < len(tiles):
                        load_x(ti + 1)
                    hTr = [f"hT{c}" for c in range(8)]

                    def inproj(col0, M):
                        s = nI()
                        for k in range(8):
                            P.mm(psI[0:M, s, 0:T], win[:, k, col0:col0 + M], hT[:, k, 0:T], k == 0, k == 7,
                                 ["win"] + hTr, [f"psI{s}"])
                        return s

                    for oc in range(4):
                        s = inproj(oc * 128, 128)
                        P.act(xrT[:, oc, xo:xo + T], psI[:, s, 0:T], AF.Copy, [f"psI{s}"], [f"xr{ti}"])
                    if lat:
                        for oc in range(4):
                            s = inproj(OFF_GATE + oc * 128, 128)
                            P.act(gx[:, 0, :], psI[:, s, :], AF.Square, [f"psI{s}"], ["gx0"])
                            P.ts(gx[:, 1, :], gx[:, 0, :], 0.044715, 1.0, ALU.mult, ALU.add, ["gx0"], ["gx1"])
                            P.tt(gx[:, 1, :], gx[:, 1, :], psI[:, s, :], ALU.mult, ["gx1", f"psI{s}"], ["gx1"])
                            P.act(gx[:, 2, :], gx[:, 1, :], AF.Tanh, ["gx1"], ["gx2"], scale=GELU_C)
                            P.stt(gst[:, oc, :], gx[:, 2, :], 1.0, psI[:, s, :], ALU.add, ALU.mult,
                                  ["gx2", f"psI{s}"], [f"gst{oc}"])
                            P.dma(gT_d[oc, :, t0:t0 + T], gst[:, oc, :], [f"gst{oc}"], [f"gT{ti}"])
                        for kq in range(2):
                            s = inproj(OFF_CQ + kq * 128, 128)
                            P.act(sqq[:, kq, :], psI[:, s, :], AF.Square, [f"psI{s}"], [f"sqq{kq}"])
                            P.ts(cqT[:, kq, :], psI[:, s, :], gqs[:, kq:kq + 1], None, ALU.mult, None,
                                 [f"psI{s}", "gqs"], [f"cqT{kq}"])
                    s = inproj(OFF_CKV, 128)
                    P.act(sqq[:, 2, 0:T], psI[:, s, 0:T], AF.Square, [f"psI{s}"], ["sqq2"])
                    P.ts(ckvg[:, 0:T], psI[:, s, 0:T], gkvs[:, 0:1], None, ALU.mult, None, [f"psI{s}", "gkvs"], ["ckvg"])
                    sA = inproj(OFF_KR, 32)
                    sB = inproj(OFF_KR + 32, 32) if lat else None
                    if lat:
                        m = nM()
                        for kq in range(2):
                            P.mm(psM[:, m, :], onesb[:], sqq[:, kq, :], kq == 0, kq == 1, ["onesb", f"sqq{kq}"], [f"psM{m}"])
                        P.ts(rq[:, 0, :], psM[:, m, :], 1.0 / 256, EPS, ALU.mult, ALU.add, [f"psM{m}"], ["rq0"])
                        P.act(rq[:, 0, :], rq[:, 0, :], AF.Sqrt, ["rq0"], ["rq0"])
                        P.recip(rq[:, 1, :], rq[:, 0, :], ["rq0"], ["rq1"])
                    m = nM()
                    P.mm(psM[:, m, 0:T], onesb[:], sqq[:, 2, 0:T], True, True, ["onesb", "sqq2"], [f"psM{m}"])
                    P.ts(rkv[:, 0, 0:T], psM[:, m, 0:T], 1.0 / 128, EPS, ALU.mult, ALU.add, [f"psM{m}"], ["rkv0"])
                    P.act(rkv[:, 0, 0:T], rkv[:, 0, 0:T], AF.Sqrt, ["rkv0"], ["rkv0"])
                    P.recip(rkv[:, 1, 0:T], rkv[:, 0, 0:T], ["rkv0"], ["rkv1"])
                    P.tt(ckvn[:, 0:T], ckvg[:, 0:T], rkv[:, 1, 0:T], ALU.mult, ["ckvg", "rkv1"], ["ckvn"])
                    if lat:
                        P.tt(tr1[0:32, 0, :], psI[0:32, sA, :], tabC[0:32, :], ALU.mult, [f"psI{sA}", tC], ["tr1a"])
                        P.tt(tr1[0:32, 1, :], psI[0:32, sB, :], tabS[0:32, :], ALU.mult, [f"psI{sB}", tS], ["tr1b"])
                        P.tt(kr_st[:, :], tr1[0:32, 0, :], tr1[0:32, 1, :], ALU.add, ["tr1a", "tr1b"], ["kr_st"], eng="pool")
                    else:
                        P.act(kr_st[:, 0:T], psI[0:32, sA, 0:T], AF.Copy, [f"psI{sA}"], ["kr_st"])
                    P.dma(krT_d[:, g0:g0 + T], kr_st[:, 0:T], ["kr_st"], [f"krT{ti}"])
                    for c in range(4):
                        m = nM()
                        P.mm(psM[:, m, 0:T], wukv[:, c * 128:(c + 1) * 128], ckvn[:, 0:T], True, True, ["wukv", "ckvn"], [f"psM{m}"])
                        P.act(kn_st[:, c, 0:T], psM[:, m, 0:T], AF.Copy, [f"psM{m}"], [f"kn_st{c}"])
                        for hh in range(2):
                            P.dma(kT_d[2 * c + hh, :, g0:g0 + T], kn_st[hh * 64:(hh + 1) * 64, c, 0:T], [f"kn_st{c}"],
                                  [f"kT{ti}"], q="sp" if hh == 0 else "act")
                    for j in range(ns):
                        m = nM()
                        P.mm(psM[:, m, :], ckvn[:, j * 128:(j + 1) * 128], wukv[:, 512:1024], True, True, ["wukv", "ckvn"], [f"psM{m}"])
                        P.act(v_st[:, j, :], psM[:, m, :], AF.Copy, [f"psM{m}"], [f"v_st{j}"])
                        P.dma(vimg_d[:, :, g0 // 128 + j, :].rearrange("h p d -> p h d"),
                              v_st[:, j, :].rearrange("p (h d) -> p h d", h=NH), [f"v_st{j}"], [f"vimg{ti}"])
                    if lat:
                        P.tt(cqs[:, 0, :], tabC, rq[:, 1, :], ALU.mult, [tC, "rq1"], ["cqs0"])
                        P.tt(cqs[:, 1, :], tabS, rq[:, 1, :], ALU.mult, [tS, "rq1"], ["cqs1"])
                        for c in range(4):
                            m = nM()
                            for kq in range(2):
                                P.mm(psM[:, m, :], wuq[:, kq, c * 128:(c + 1) * 128], cqT[:, kq, :], kq == 0, kq == 1,
                                     ["wuq", f"cqT{kq}"], [f"psM{m}"])
                            P.tt(qn_st[:, c, :], psM[:, m, :], rq[:, 1, :], ALU.mult, [f"psM{m}", "rq1"], [f"qn_st{c}"])
                            for hh in range(2):
                                P.dma(qT_d[2 * c + hh, 0:64, t0:t0 + T], qn_st[hh * 64:(hh + 1) * 64, c, :], [f"qn_st{c}"],
                                      [f"qT{ti}"], q="sp" if hh == 0 else "act")
                        for c in range(2):
                            mA = nM()
                            for kq in range(2):
                                P.mm(psM[:, mA, :], wuq[:, kq, 512 + c * 128:512 + (c + 1) * 128], cqT[:, kq, :], kq == 0, kq == 1,
                                     ["wuq", f"cqT{kq}"], [f"psM{mA}"])
                            mB = nM()
                            for kq in range(2):
                                P.mm(psM[:, mB, :], wuq[:, kq, 768 + c * 128:768 + (c + 1) * 128], cqT[:, kq, :], kq == 0, kq == 1,
                                     ["wuq", f"cqT{kq}"], [f"psM{mB}"])
                            P.tt(tr1[:, 0, :], psM[:, mA, :], cqs[:, 0, :], ALU.mult, [f"psM{mA}", "cqs0"], ["tr1a"])
                            P.tt(tr1[:, 1, :], psM[:, mB, :], cqs[:, 1, :], ALU.mult, [f"psM{mB}", "cqs1"], ["tr1b"])
                            P.tt(qr_st[:, c, :], tr1[:, 0, :], tr1[:, 1, :], ALU.add, ["tr1a", "tr1b"], [f"qr_st{c}"], eng="pool")
                            for hh in range(4):
                                P.dma(qT_d[4 * c + hh, 64:96, t0:t0 + T], qr_st[hh * 32:(hh + 1) * 32, c, :], [f"qr_st{c}"],
                                      [f"qT{ti}"], q="sp" if hh % 2 == 0 else "act")
                if dbgxr_d is not None:
                    for c in range(4):
                        P.dma(dbgxr_d[c], xrT[:, c, :], [f"xr{ti}" for ti in range(len(tiles))] + ["xrpad"], ["dbgxr"])
                P.finalize(block)
            if stop == "PA":
                return nc
            stW.close()

            with ExitStack() as st, nc.Block() as block:
                P = Prog(sy)
                P.act_ord = True
                wg = sb(st, "wg", [128, 16, 128], BF16)
                lv = sb(st, "lv", [128, 24], F32)
                lcw = sb(st, "lcw", [128, 16], F32)
                lcb = sb(st, "lcb", [128, 4], F32)
                lc = sb(st, "lc", [128, 4, 8], F32)
                spt = sb(st, "spt", [128, 8], F32)
                xc = sb(st, "xc", [128, 2, 4, 512], BF16)
                wk = sb(st, "wk", [128, 2, 4, 4, 512], F32)
                car = sb(st, "car", [128, 8], F32)
                hst = sb(st, "hst", [128, 4, 512], BF16)
                hfl = sb(st, "hfl", [128, 2, 4, 512], BF16)
                gl = sb(st, "gl", [128, 2, 4, 512], BF16)
                yst = sb(st, "yst", [128, 2, 4, 512], BF16)
                psG = ps(st, "psG", [128, 6, 512], F32)
                psC = ps(st, "psC", [128, 2, 512], F32)
                dg = sb(st, "dg", [128, 16, 128], BF16)

                P.dma(wg[:], lbd_d.rearrange("i p m -> p i m"), (), ["wg"], q="pool")
                P.dma(lv[:], lvec_d, (), ["lv"])
                P.dma(lcw[:], lcw_d, (), ["lcw"])
                P.dma(lcb[:], lcb_d, (), ["lcb"])
                P.act(spt[:], lv[:, 16:24], AF.Exp, ["lv"], ["spt"], scale=-1.0)
                P.act(spt[:], spt[:], AF.Ln, ["spt"], ["spt"], bias=1.0)
                P.ts(lc[:, 3, :], spt[:], -8.0, None, ALU.mult, None, ["spt"], ["lc3"])
                P.ts(lc[:, 2, :], spt[:], -4.0, None, ALU.mult, None, ["spt"], ["lc2"])
                P.ts(lc[:, 0, :], lv[:, 0:8], 0.5, None, ALU.mult, None, ["lv"], ["lc0"])
                P.ts(lc[:, 1, :], lv[:, 8:16], 0.5, None, ALU.mult, None, ["lv"], ["lc1"])
                lcr = ["lc0", "lc1", "lc2", "lc3"]
                ntl = len(tiles)
                for i16 in range(16):
                    P.ts(dg[:, i16, :], ident[:], lcw[:, i16:i16 + 1], None, ALU.mult, None, ["lcw"], ["dg"])
                gcnt = [0]
                ccnt = [0]

                def lru_pass(d):
                    rev = d == 1
                    order = [0] + (list(range(ntl - 1, 0, -1)) if rev else list(range(1, ntl)))
                    for oi, ti in enumerate(order):
                        kind, t0, T = tiles[ti]
                        lat = kind == "l"
                        xo = (XO_L if lat else XO_C) + t0
                        sl = oi % 2
                        if rev and lat:
                            P.dma(hfl[:, sl, :, :], hf_d[:, :, t0:t0 + T].rearrange("c p t -> p c t"), [f"hf{ti}"], [f"hfl{sl}"])
                            P.dma(gl[:, sl, :, :], gT_d[:, :, t0:t0 + T].rearrange("c p t -> p c t"), (), [f"gl{sl}"], q="act")
                        if not rev:
                            nb = [f"xr{ti}"]
                            if lat and ti - 1 >= 1:
                                nb.append(f"xr{ti - 1}")
                            if lat and ti + 1 < ntl:
                                nb.append(f"xr{ti + 1}")
                            for c in range(4):
                                cs = ccnt[0] % 2
                                ccnt[0] += 1
                                for k in range(4):
                                    P.mm(psC[:, cs, 0:T], dg[:, k * 4 + c, :], xrT[:, c, xo + k - 2:xo + k - 2 + T], k == 0, k == 3,
                                         nb + ["dg", "xrpad"], [f"psC{cs}"])
                                P.act(xc[:, sl, c, 0:T], psC[:, cs, 0:T], AF.Identity, [f"psC{cs}", "lcb"], [f"xc{sl}{c}"],
                                      bias=lcb[:, c:c + 1])
                            if oi >= 1:
                                pti = order[oi - 1]
                                pk, pt0, pT = tiles[pti]
                                pxo = (XO_L if pk == "l" else XO_C) + pt0
                                for c in range(4):
                                    P.cp(xrT[:, c, pxo:pxo + pT], xc[:, 1 - sl, c, 0:pT], [f"xc{1 - sl}{c}"], [f"xr{pti}"], eng="pool")
                            xsrc = [xc[:, sl, c, 0:T] for c in range(4)]
                            xtok = [f"xc{sl}{c}" for c in range(4)]
                        else:
                            xsrc = [xrT[:, c, xo:xo + T] for c in range(4)]
                            xtok = [f"xr{ti}"] * 4
                        col = d * 4
                        for c in range(4):
                            for gi in range(2):
                                gs_ = gcnt[0] % 6
                                gcnt[0] += 1
                                P.mm(psG[:, gs_, 0:T], wg[:, (d * 2 + gi) * 4 + c, :], xsrc[c], True, True,
                                     ["wg", xtok[c]], [f"psG{gs_}"])
                                P.act(wk[:, sl, c, gi, 0:T], psG[:, gs_, 0:T], AF.Tanh, [f"psG{gs_}"] + lcr, [f"wk{sl}{c}{gi}"],
                                      scale=0.5, bias=lc[:, gi, col + c:col + c + 1])
                        for c in range(4):
                            P.act(wk[:, sl, c, 2, 0:T], wk[:, sl, c, 0, 0:T], AF.Exp, [f"wk{sl}{c}0"] + lcr, [f"wk{sl}{c}2"],
                                  scale=lc[:, 2, col + c:col + c + 1], bias=lc[:, 2, col + c:col + c + 1])
                            P.tt(wk[:, sl, c, 3, 0:T], wk[:, sl, c, 2, 0:T], wk[:, sl, c, 2, 0:T], ALU.mult, [f"wk{sl}{c}2"], [f"wk{sl}{c}3"], eng="pool")
                        for c in range(4):
                            P.act(wk[:, sl, c, 3, 0:T], wk[:, sl, c, 3, 0:T], AF.Sqrt, [f"wk{sl}{c}3"], [f"wk{sl}{c}3"], scale=-1.0, bias=1.0)
                        for c in range(4):
                            u_ = wk[:, sl, c, 1, 0:T]
                            P.stt(u_, u_, 1.0, xsrc[c], ALU.add, ALU.mult, [f"wk{sl}{c}1", xtok[c]], [f"wk{sl}{c}1"])
                            P.stt(u_, u_, 0.5, wk[:, sl, c, 3, 0:T], ALU.mult, ALU.mult, [f"wk{sl}{c}1", f"wk{sl}{c}3"], [f"wk{sl}{c}1"])
                            h_ = wk[:, sl, c, 0, 0:T]
                            a_ = wk[:, sl, c, 2, 0:T]
                            init = 0.0 if oi == 0 else car[:, col + c:col + c + 1]
                            rd = [f"wk{sl}{c}1", f"wk{sl}{c}2"] + ([] if oi == 0 else [f"car{col + c}"])
                            if rev:
                                P.scan(h_[:, ::-1], a_[:, ::-1], u_[:, ::-1], init, rd, [f"wk{sl}{c}0"])
                                P.cp(car[:, col + c:col + c + 1], h_[:, 0:1], [f"wk{sl}{c}0"], [f"car{col + c}"], eng="pool")
                            else:
                                P.scan(h_, a_, u_, init, rd, [f"wk{sl}{c}0"])
                                P.cp(car[:, col + c:col + c + 1], h_[:, T - 1:T], [f"wk{sl}{c}0"], [f"car{col + c}"], eng="pool")
                            if lat and not rev:
                                P.cp(hst[:, c, :], h_, [f"wk{sl}{c}0"], [f"hst{c}"], eng="pool")
                            if lat and rev:
                                P.tt(h_, h_, hfl[:, sl, c, :], ALU.add, [f"wk{sl}{c}0", f"hfl{sl}"], [f"wk{sl}{c}0"])
                                P.stt(yst[:, sl, c, :], h_, 0.5, gl[:, sl, c, :], ALU.mult, ALU.mult, [f"wk{sl}{c}0", f"gl{sl}"], [f"yst{sl}{c}"])
                        if lat and not rev:
                            P.dma(hf_d[:, :, t0:t0 + T].rearrange("c p t -> p c t"), hst[:], [f"hst{c}" for c in range(4)], [f"hf{ti}"])
                        if lat and rev:
                            P.dma(yT_d[0:4, :, t0:t0 + T].rearrange("c p t -> p c t"), yst[:, sl, :, :],
                                  [f"yst{sl}{c}" for c in range(4)], [f"yT{ti}"])
                    if not rev:
                        pti = order[-1]
                        pk, pt0, pT = tiles[pti]
                        pxo = (XO_L if pk == "l" else XO_C) + pt0
                        sl = (len(order) - 1) % 2
                        for c in range(4):
                            P.cp(xrT[:, c, pxo:pxo + pT], xc[:, sl, c, 0:pT], [f"xc{sl}{c}"], [f"xr{pti}"], eng="pool")

                lru_pass(0)
                lru_pass(1)
                if dbgxr_d is not None:
                    for c in range(4):
                        P.dma(dbgxr_d[c], xrT[:, c, :], [f"xr{ti}" for ti in range(len(tiles))] + ["xrpad"], ["dbgxr"])
                P.finalize(block)
            if stop == "PB":
                return nc

        stX.close()

        with ExitStack() as st, nc.Block() as block:
            P = Prog(sy)
            NQT = S // 512
            kTs = sb(st, "kTs", [128, 2, NK], BF16)
            vs = sb(st, "vs", [128, 2, NKC, 65], BF16)
            qTs = sb(st, "qTs", [128, 2, S], BF16)
            pT = sb(st, "pT", [128, 3, 2, 512], BF16)
            lsb = sb(st, "lsb", [128, 512], F32)
            rl = sb(st, "rl", [64, 512], F32)
            ost = sb(st, "ost", [64, 2, 512], BF16)
            psS = ps(st, "psS", [128, 2, 2, 512], F32)
            psL = ps(st, "psL", [64, 2, 512], F32)
            psO = ps(st, "psO", [128, 2, 512], F32)
            for sl in range(2):
                P.memset(vs[:, sl, :, 64:65], 1.0, [f"vone{sl}"])

            def load_head(h):
                sl = h % 2
                P.dma(kTs[0:64, sl, :], kT_d[h], (), [f"kTs{sl}"])
                P.dma(kTs[64:96, sl, :], krT_d, (), [f"kTs{sl}"], q="act")
                P.dma(vs[:, sl, :, 0:64], vimg_d[h], (), [f"vs{sl}"])
                P.dma(qTs[0:96, sl, :], qT_d[h], (), [f"qTs{sl}"], q="act")

            pairs = [(h, qt, kp) for h in range(NH) for qt in range(NQT) for kp in range(NKC // 2)]
            npairs = len(pairs)
            PPH = NQT * (NKC // 2)

            def emit_qk(i):
                h, qt, kp = pairs[i]
                hs = h % 2
                for j in range(2):
                    kc = 2 * kp + j
                    P.mm(psS[:, i % 2, j, :], kTs[0:96, hs, kc * 128:(kc + 1) * 128], qTs[0:96, hs, qt * 512:(qt + 1) * 512],
                         True, True, [f"kTs{hs}", f"qTs{hs}"], [f"psS{i % 2}{j}"])

            def emit_exp(i):
                P.act(pT[:, i % 3, :, :], psS[:, i % 2, :, :], AF.Exp, [f"psS{i % 2}0", f"psS{i % 2}1"], [f"pT{i % 3}"], scale=MLA_SCALE)

            def emit_pv(i):
                h, qt, kp = pairs[i]
                hs = h % 2
                osl = (h * NQT + qt) % 2
                for j in range(2):
                    kc = 2 * kp + j
                    P.mm(psO[0:65, osl, :], vs[:, hs, kc, 0:65], pT[:, i % 3, j, :], kc == 0, kc == NKC - 1,
                         [f"vs{hs}", f"vone{hs}", f"pT{i % 3}"], [f"psO{osl}"])
                if kp == NKC // 2 - 1:
                    P.cp(lsb[64:65, :], psO[64:65, osl, :], [f"psO{osl}"], ["lsb"])
                    P.mm(psL[:, osl, :], onesf[64:65, 0:64], lsb[64:65, :], True, True, ["lsb", "onesf"], [f"psL{osl}"])
                    P.recip(rl[:, :], psL[:, osl, :], [f"psL{osl}"], ["rl"])
                    P.tt(ost[:, osl, :], psO[0:64, osl, :], rl[:, :], ALU.mult, [f"psO{osl}", "rl"], [f"ost{osl}"])
                    P.dma(yT_d[4 + h // 2, (h % 2) * 64:(h % 2) * 64 + 64, qt * 512:(qt + 1) * 512], ost[:, osl, :],
                          [f"ost{osl}"], [f"yTm{h}_{qt}"], q="sp" if osl == 0 else "act")
                if (i + 1) % PPH == 0 and h + 2 < NH:
                    load_head(h + 2)

            load_head(0)
            if NH > 1:
                load_head(1)
            emit_qk(0)
            if npairs > 1:
                emit_qk(1)
            for i in range(npairs):
                emit_exp(i)
                if i + 2 < npairs:
                    emit_qk(i + 2)
                emit_pv(i)
            P.finalize(block)
        if stop == "PC":
            return nc

        stW2 = ExitStack()
        wup = sb(stW2, "wup", [128, 8, 2 * DFF], BF16)

        with ExitStack() as st, nc.Block() as block:
            P = Prog(sy)
            wout = sb(st, "wout", [128, 8, D], BF16)
            ycT = sb(st, "ycT", [128, 2, 8, 512], BF16)
            xs = sb(st, "xs", [128, 3, D], F32)
            t1 = sb(st, "t1", [128, 3, D], F32)
            x1 = sb(st, "x1", [128, 3, D], F32)
            junk = sb(st, "junk", [128, D], BF16)
            xh1 = sb(st, "xh1", [128, 4, D], BF16)
            h2T = sb(st, "h2T", [128, 2, 8, 512], BF16)
            sts = sb(st, "sts", [128, 2, 3, 4], F32)
            zt = sb(st, "zt", [128, 8, 1], BF16)
            psY = ps(st, "psY", [128, 3, 2, 512], F32)
            psT2 = ps(st, "psT2", [128, 2, 2, 512], BF16)
            P.dma(wout[:], wout_d.rearrange("(k p) f -> p k f", p=128), (), ["wout"], q="pool")
            wupv = wup_d.rearrange("(k p) f -> p k f", p=128)
            for q4 in range(4):
                P.dma(wup[:, :, q4 * 1408:(q4 + 1) * 1408], wupv[:, :, q4 * 1408:(q4 + 1) * 1408], (), ["wup"], q="pool")
            P.memset(zt[:], 0.0, ["zt"])
            P.dma(h2_d[:, :, 0:1].rearrange("c p t -> p c t"), zt[:], ["zt"], ["h2pad0"], slow=True)
            P.dma(h2_d[:, :, S + 1:S + 2].rearrange("c p t -> p c t"), zt[:], ["zt"], ["h2pad1"], slow=True)
            nsub = 0
            for ti in range(NT):
                t0 = ti * 512
                ysl = ti % 2
                P.dma(ycT[:, ysl, :, :], yT_d[:, :, t0:t0 + 512].rearrange("c p t -> p c t"), (), [f"ycT{ysl}"])
                for j in range(4):
                    xsl = nsub % 3
                    psl = nsub % 3
                    nsub += 1
                    r0 = t0 + j * 128
                    P.dma(xs[:, xsl, :], x_d[r0:r0 + 128, :], (), [f"xs{xsl}"], q="act")
                    for n in range(2):
                        for k in range(8):
                            P.mm(psY[:, psl, n, :], ycT[:, ysl, k, j * 128:(j + 1) * 128], wout[:, k, n * 512:(n + 1) * 512],
                                 k == 0, k == 7, [f"ycT{ysl}", "wout"], [f"psY{psl}{n}"])
                    pr = [f"psY{psl}0", f"psY{psl}1"]
                    P.act(junk[:], psY[:, psl, :, :].rearrange("p a b -> p (a b)"), AF.Square, pr, ["junk", "ssy"], accum=sts[:, 0, 0, j:j + 1])
                    P.ts(sts[:, 0, 1, j:j + 1], sts[:, 0, 0, j:j + 1], 1.0 / D, EPS, ALU.mult, ALU.add, ["ssy"], ["msy"])
                    P.act(sts[:, 0, 1, j:j + 1], sts[:, 0, 1, j:j + 1], AF.Sqrt, ["msy"], ["msy"])
                    P.recip(sts[:, 0, 2, j:j + 1], sts[:, 0, 1, j:j + 1], ["msy"], ["rsy"])
                    P.stt(t1[:, psl, :], psY[:, psl, :, :].rearrange("p a b -> p (a b)"), sts[:, 0, 2, j:j + 1], gpbc[:, 0, :],
                          ALU.mult, ALU.mult, pr + ["rsy"], [f"t1{psl}"])
                    P.tt(x1[:, xsl, :], t1[:, psl, :], xs[:, xsl, :], ALU.add, [f"t1{psl}", f"xs{xsl}"], [f"x1{xsl}"], eng="pool")
                    P.dma(x1_d[r0:r0 + 128, :], x1[:, xsl, :], [f"x1{xsl}"], [f"x1d{ti}_{j}"])
                    P.act(junk[:], x1[:, xsl, :], AF.Square, [f"x1{xsl}"], ["junk", "ss1"], accum=sts[:, 1, 0, j:j + 1])
                    P.ts(sts[:, 1, 1, j:j + 1], sts[:, 1, 0, j:j + 1], 1.0 / D, EPS, ALU.mult, ALU.add, ["ss1"], ["ms1"])
                    P.act(sts[:, 1, 1, j:j + 1], sts[:, 1, 1, j:j + 1], AF.Sqrt, ["ms1"], ["ms1"])
                    P.recip(sts[:, 1, 2, j:j + 1], sts[:, 1, 1, j:j + 1], ["ms1"], ["rs1"])
                    P.ts(xh1[:, j, :], x1[:, xsl, :], sts[:, 1, 2, j:j + 1], None, ALU.mult, None, [f"x1{xsl}", "rs1"], [f"xh1{j}"])
                hsl = ti % 2
                for cp in range(4):
                    sl = cp % 2
                    for j in range(4):
                        for cc in range(2):
                            c = cp * 2 + cc
                            P.tr(psT2[:, sl, cc, j * 128:(j + 1) * 128], xh1[:, j, c * 128:(c + 1) * 128], ident[:],
                                 [f"xh1{j}"], [f"psT{sl}"])
                    for cc in range(2):
                        c = cp * 2 + cc
                        P.act(h2T[:, hsl, c, :], psT2[:, sl, cc, :], AF.Identity, [f"psT{sl}"], [f"h2T{hsl}"],
                              scale=modv[:, 4, c:c + 1], bias=modv[:, 5, c:c + 1])
                P.dma(h2_d[:, :, 1 + t0:1 + t0 + 512].rearrange("c p t -> p c t"), h2T[:, hsl, :, :], [f"h2T{hsl}"], [f"h2d{ti}"])
            P.finalize(block)
        if stop == "PD1":
            return nc

        with ExitStack() as st, nc.Block() as block:
            P = Prog(sy)
            wdn = sb(st, "wdn", [128, FC, D], BF16)
            fcw = sb(st, "fcw", [128, 3 * 44], F32)
            fcb = sb(st, "fcb", [128, 44], F32)
            h2w = sb(st, "h2w", [128, 8, 514], BF16)
            upr = sb(st, "upr", [128, 2, 514], F32)
            ac = sb(st, "ac", [128, 3, 2, 512], F32)
            actT = sb(st, "actT", [128, FC, 512], BF16)
            x1t = sb(st, "x1t", [128, D], F32)
            t2 = sb(st, "t2", [128, D], F32)
            st2 = sb(st, "st2", [128, 3, 4], F32)
            psU = ps(st, "psU", [128, 2, 512], F32)
            psH = ps(st, "psH", [128, 2, 512], F32)
            psF = ps(st, "psF", [128, 2, 2, 512], F32)
            P.dma(wdn[:], wdn_d.rearrange("(k p) f -> p k f", p=128), (), ["wdn"], q="pool")
            P.dma(fcw[:], fcw_d, (), ["fcw"])
            P.dma(fcb[:], fcb_d, (), ["fcb"])
            nu = 0
            nf = 0
            for ti in range(NT):
                t0 = ti * 512
                P.dma(h2w[:], h2_d[:, :, t0:t0 + 514].rearrange("c p t -> p c t"), (), ["h2w"])

                def gate(fc_, extra):
                    b_ = fc_ % 3
                    P.act(ac[:, b_, 1, :], ac[:, b_, 1, :], AF.Silu, [f"ac{b_}1"] + extra, [f"ac{b_}1"])
                    P.tt(actT[:, fc_, :], ac[:, b_, 1, :], ac[:, b_, 0, :], ALU.mult, [f"ac{b_}0", f"ac{b_}1"], [f"actT{fc_}"])

                for fc in range(FC):
                    ab = fc % 3
                    for part in range(2):
                        fi = part * FC + fc
                        us = nu % 2
                        hs_ = nu % 2
                        nu += 1
                        for k in range(8):
                            P.mm(psU[:, us, :], wup[:, k, fi * 128:(fi + 1) * 128], h2w[:, k, 1:513], k == 0, k == 7,
                                 ["wup", "h2w"], [f"psU{us}"])
                        for k in range(8):
                            P.mm(psH[:, hs_, 0:2], wup[:, k, fi * 128:(fi + 1) * 128], h2w[:, k, 0:514:513], k == 0, k == 7,
                                 ["wup", "h2w"], [f"psH{hs_}"])
                        P.act(ac[:, ab, part, :], psU[:, us, :], AF.Identity, [f"psU{us}", "fcw", "fcb"], [f"ac{ab}{part}"],
                              scale=fcw[:, 44 + fi:45 + fi], bias=fcb[:, fi:fi + 1])
                        P.act(upr[:, hs_, 1:513], psU[:, us, :], AF.Copy, [f"psU{us}"],
                              [f"upr{hs_}"] + ([f"evac{fc}"] if part == 1 else []))
                        P.cp(upr[:, hs_, 0:514:513], psH[:, hs_, 0:2], [f"psH{hs_}"], [f"upr{hs_}"])
                        P.stt(ac[:, ab, part, :], upr[:, hs_, 0:512], fcw[:, fi:fi + 1], ac[:, ab, part, :], ALU.mult, ALU.add,
                              [f"upr{hs_}", f"ac{ab}{part}", "fcw"], [f"ac{ab}{part}"])
                        P.stt(ac[:, ab, part, :], upr[:, hs_, 2:514], fcw[:, 88 + fi:89 + fi], ac[:, ab, part, :], ALU.mult, ALU.add,
                              [f"upr{hs_}", f"ac{ab}{part}", "fcw"], [f"ac{ab}{part}"])
                    if fc >= 1:
                        gate(fc - 1, [f"evac{fc}"])
                gate(FC - 1, [])
                aT = [f"actT{fc}" for fc in range(FC)]
                for j in range(4):
                    fs = nf % 2
                    nf += 1
                    r0 = t0 + j * 128
                    P.dma(x1t[:], x1_d[r0:r0 + 128, :], (), ["x1t"], q="act")
                    for n in range(2):
                        for k in range(FC):
                            P.mm(psF[:, fs, n, :], actT[:, k, j * 128:(j + 1) * 128], wdn[:, k, n * 512:(n + 1) * 512],
                                 k == 0, k == FC - 1, aT + ["wdn"], [f"psF{fs}{n}"])
                    pr = [f"psF{fs}0", f"psF{fs}1"]
                    pv = psF[:, fs, :, :].rearrange("p a b -> p (a b)")
                    P.act(t2[:], pv, AF.Square, pr, ["t2", "ssf"], accum=st2[:, 0, j:j + 1])
                    P.ts(st2[:, 1, j:j + 1], st2[:, 0, j:j + 1], 1.0 / D, EPS, ALU.mult, ALU.add, ["ssf"], ["msf"])
                    P.act(st2[:, 1, j:j + 1], st2[:, 1, j:j + 1], AF.Sqrt, ["msf"], ["msf"])
                    P.recip(st2[:, 2, j:j + 1], st2[:, 1, j:j + 1], ["msf"], ["rsf"])
                    P.stt(t2[:], pv, st2[:, 2, j:j + 1], gpbc[:, 1, :], ALU.mult, ALU.mult, pr + ["rsf"], ["t2"])
                    P.tt(t2[:], t2[:], x1t[:], ALU.add, ["t2", "x1t"], ["t2"], eng="pool")
                    P.dma(out_d[r0:r0 + 128, :], t2[:], ["t2"], [f"out{ti}_{j}"])
            P.finalize(block)
        stW2.close()
    return nc


def kernel(**inputs):
    inp = {k: np.asarray(v) for k, v in inputs.items()}
    B, S = inp["x"].shape[0], inp["x"].shape[1]
    maps = host_prep(inp, S)
    nc = build(S)
    res = run_bass_kernel_spmd(nc, maps, core_ids=list(range(B)))
    out = np.stack([np.asarray(r["out"], dtype=np.float32) for r in res.results], axis=0)
    return out
```

```python
import numpy as np
from contextlib import ExitStack
import ml_dtypes
import concourse.bass as bass
import concourse.mybir as mybir
from concourse.bass_utils import run_bass_kernel_spmd
from concourse.alu_op_type import AluOpType as ALU

F32 = mybir.dt.float32
BF16 = mybir.dt.bfloat16
AF = mybir.ActivationFunctionType
AX = mybir.AxisListType

D = 1024
KC = 8
CT = 256
LRU_W = 512
NH = 8
DFF = 2816
FC = DFF // 128
EPS = 1e-6
MLA_SCALE = 96 ** -0.5
GELU_C = 0.7978845608028654
OFF_GATE, OFF_CQ, OFF_CKV, OFF_KR = 512, 1024, 1280, 1408
ROLL = 4096
KRING = 8


class Sync:
    def __init__(self, nc, es):
        self.nc, self.es = nc, es
        self.csem = {e: [] for e in ("pe", "act", "dve", "pool")}
        self.cnt = {e: 0 for e in ("pe", "act", "dve", "pool")}
        self.ring = {q: [es.enter_context(nc.semaphore(f"dq_{q}{i}")) for i in range(KRING)]
                     for q in ("sp", "act", "pool")}
        self.dcnt = {q: 0 for q in ("sp", "act", "pool")}
        self.known = {f: {e: 0 for e in ("pe", "act", "dve", "pool")} for f in ("pe", "act", "dve", "pool", "sp")}
        self.knownd = {f: {} for f in ("pe", "act", "dve", "pool", "sp")}

    def sem_for(self, eng, val):
        idx = (val - 1) // ROLL
        while len(self.csem[eng]) <= idx:
            self.csem[eng].append(self.es.enter_context(self.nc.semaphore(f"c_{eng}{len(self.csem[eng])}")))
        return self.csem[eng][idx], (val - 1) % ROLL + 1


class Ins:
    __slots__ = ("eng", "fn", "r", "w", "w0", "dma", "deps", "inc", "val", "dsem", "dval", "waits", "raw", "dur", "lat")

    def __init__(self, eng, fn, r, w, dma, dur=100.0, lat=0.0, w0=()):
        self.eng, self.fn, self.r, self.w, self.dma = eng, fn, r, w, dma
        self.w0 = w0
        self.dur, self.lat = dur, lat
        self.deps = {}
        self.inc = False
        self.val = 0
        self.waits = []


class Prog:
    def __init__(self, sy):
        self.sy = sy
        self.I = []

    limit = None

    def add(self, eng, fn, r=(), w=(), dma=False, dur=100.0, lat=0.0):
        if Prog.limit is not None and len(self.I) >= Prog.limit:
            return
        r, w0 = tuple(r), tuple(w)
        w = w0 + tuple(t for t in r if t.startswith("ps") and t not in w0)
        self.I.append(Ins(eng, fn, r, w, dma, dur, lat, w0))

    @staticmethod
    def _n(ap):
        n = 1
        for d in ap.shape[1:]:
            n *= d
        return n

    def mm(self, out, lhsT, rhs, start, stop, r, w):
        n = self._n(rhs)
        dur = max(n, 64) / 2.0 * (4.0 if rhs.dtype == F32 else 1.0) + 12.0
        self.add("pe", lambda e: e.matmul(out, lhsT=lhsT, rhs=rhs, start=start, stop=stop), r, w, dur=dur)

    def tr(self, out, in_, ident, r, w):
        self.add("pe", lambda e: e.transpose(out=out, in_=in_, identity=ident), r, w, dur=80.0)

    act_ord = False

    def act(self, out, in_, func, r, w, scale=1.0, bias=0.0, accum=None, eng="act"):
        dur = (self._n(in_) + 230) / 1.2 + (90.0 if accum is not None else 0.0)
        if self.act_ord:
            w = list(w) + ["ord:act"]
        if accum is None:
            self.add(eng, lambda e: e.activation(out=out, in_=in_, func=func, bias=bias, scale=scale), r, w, dur=dur)
        else:
            self.add(eng, lambda e: e.activation(out=out, in_=in_, func=func, bias=bias, scale=scale,
                                                 accum_out=accum), r, w, dur=dur)

    def _vd(self, n, eng, mult=1.0):
        return (n * mult / 0.96 + 70.0) * (2.0 if eng == "pool" else 1.0)

    def ts(self, out, in0, s1, s2, op0, op1, r, w, eng="dve"):
        dur = self._vd(self._n(in0), eng, 0.7)
        if op1 is None:
            self.add(eng, lambda e: e.tensor_scalar(out=out, in0=in0, scalar1=s1, scalar2=None, op0=op0), r, w, dur=dur)
        else:
            self.add(eng, lambda e: e.tensor_scalar(out=out, in0=in0, scalar1=s1, scalar2=s2, op0=op0, op1=op1), r, w, dur=dur)

    def tt(self, out, in0, in1, op, r, w, eng="dve"):
        self.add(eng, lambda e: e.tensor_tensor(out=out, in0=in0, in1=in1, op=op), r, w, dur=self._vd(self._n(in0), eng))

    def stt(self, out, in0, scalar, in1, op0, op1, r, w):
        self.add("dve", lambda e: e.scalar_tensor_tensor(out=out, in0=in0, scalar=scalar, in1=in1, op0=op0, op1=op1), r, w,
                 dur=self._vd(self._n(in0), "dve"))

    def cp(self, out, in_, r, w, eng="dve"):
        self.add(eng, lambda e: e.tensor_copy(out=out, in_=in_), r, w, dur=self._vd(self._n(in_), eng, 0.7))

    def recip(self, out, in_, r, w):
        self.add("dve", lambda e: e.reciprocal(out=out, in_=in_), r, w, dur=self._vd(self._n(in_), "dve", 8.0))

    def scan(self, out, a, u, initial, r, w):
        self.add("dve", lambda e: e.tensor_tensor_scan(out=out, data0=a, data1=u, initial=initial,
                                                       op0=ALU.mult, op1=ALU.add), r, w, dur=self._vd(self._n(a), "dve", 2.0))

    def memset(self, ap, val, w, eng="pool"):
        self.add(eng, lambda e: e.memset(ap, val), (), w, dur=self._vd(self._n(ap), eng, 0.5))

    def dma(self, out, in_, r, w, q="sp", slow=False):
        nbytes = self._n(out) * out.shape[0] * (4 if out.dtype == F32 else 2)
        lat = 2000.0 + nbytes / 60.0
        if slow:
            self.add(q, lambda e: e.dma_start(out=out, in_=in_, allow_slow_non_contiguous=True), r, w, dma=True, dur=60.0, lat=lat)
        else:
            self.add(q, lambda e: e.dma_start(out=out, in_=in_), r, w, dma=True, dur=60.0, lat=lat)

    schedule = True

    def _schedule(self, I):
        n = len(I)
        succ = [[] for _ in range(n)]
        npred = [0] * n
        for i, ins in enumerate(I):
            npred[i] = len(ins.deps)
            for p in ins.deps:
                succ[p].append(i)
        engs = ("pe", "act", "dve", "pool", "sp")
        te = {e: 0.0 for e in engs}
        fin = [0.0] * n
        rdy = [0.0] * n
        ready = {e: [] for e in engs}
        import heapq
        for i in range(n):
            if npred[i] == 0:
                heapq.heappush(ready[I[i].eng], (0.0, i))
        order = []
        WIN = 6
        while len(order) < n:
            best = None
            for e in engs:
                h = ready[e]
                if not h:
                    continue
                cand = heapq.nsmallest(WIN, h)
                for rt, i in cand:
                    st_ = max(te[e], rt)
                    key = (st_, i)
                    if best is None or key < best[0]:
                        best = (key, e, (rt, i))
            (st_, i), e, item = best
            ready[e].remove(item)
            heapq.heapify(ready[e])
            ins = I[i]
            te[e] = st_ + ins.dur
            fin[i] = st_ + ins.dur + ins.lat
            order.append(i)
            for sidx in succ[i]:
                npred[sidx] -= 1
                rdy[sidx] = max(rdy[sidx], fin[i] + (0.0 if I[sidx].eng == e and not ins.dma else 400.0))
                if npred[sidx] == 0:
                    heapq.heappush(ready[I[sidx].eng], (rdy[sidx], sidx))
        pos = {old: new for new, old in enumerate(order)}
        out = []
        for old in order:
            ins = I[old]
            ins.deps = {pos[p]: raw for p, raw in ins.deps.items()}
            out.append(ins)
        self.est = max(fin) if n else 0.0
        return out

    def finalize(self, block):
        sy = self.sy
        I = self.I
        last_w, readers = {}, {}
        for idx, ins in enumerate(I):
            deps = ins.deps
            for t in ins.r:
                if t in last_w:
                    deps[last_w[t]] = True
            for t in ins.w:
                if t in last_w:
                    deps.setdefault(last_w[t], False)
                for ri in readers.get(t, ()):
                    deps.setdefault(ri, False)
            deps.pop(idx, None)
            for t in ins.r:
                readers.setdefault(t, []).append(idx)
            for t in ins.w:
                last_w[t] = idx
                readers[t] = []
        if Prog.schedule:
            I = self._schedule(I)
            self.I = I
        need = []
        for idx, ins in enumerate(I):
            lst = []
            for pi, raw in ins.deps.items():
                p = I[pi]
                if p.dma:
                    lst.append(pi)
                    continue
                if p.eng == ins.eng and not ins.dma:
                    if ins.eng == "pe":
                        continue
                    if not (any((t in p.r or t in p.w0) and not t.startswith("ord:") for t in ins.w0)
                            or any(t in p.w0 and not t.startswith("ord:") for t in ins.r)):
                        continue
                lst.append(pi)
                p.inc = True
            need.append(lst)
        lastc = {}
        for idx, ins in enumerate(I):
            if not ins.dma and ins.fn is not None:
                lastc[ins.eng] = idx
        for e, idx in lastc.items():
            I[idx].inc = True
        for idx, ins in enumerate(I):
            f = ins.eng
            waits = []
            best = {}
            for pi in need[idx]:
                p = I[pi]
                if p.dma:
                    key = (p.eng, id(p.dsem))
                    if sy.knownd[f].get(key, 0) < p.dval:
                        sy.knownd[f][key] = p.dval
                        waits.append((p.dsem, p.dval))
                else:
                    best[p.eng] = max(best.get(p.eng, 0), p.val)
            for e, v in best.items():
                if sy.known[f][e] < v:
                    sy.known[f][e] = v
                    waits.append(sy.sem_for(e, v))
            if ins.dma:
                q = ins.eng
                n = sy.dcnt[q]
                sy.dcnt[q] += 1
                ins.dsem = sy.ring[q][n % KRING]
                ins.dval = 16 * (n // KRING + 1)
                if n >= KRING:
                    key = (q, id(ins.dsem))
                    if sy.knownd[f].get(key, 0) < ins.dval - 16:
                        sy.knownd[f][key] = ins.dval - 16
                        waits.append((ins.dsem, ins.dval - 16))
            elif ins.inc:
                sy.cnt[f] += 1
                ins.val = sy.cnt[f]
            ins.waits = waits
        bar = {}
        for f in ("pe", "act", "dve", "pool", "sp"):
            waits = []
            for e, idx in lastc.items():
                if e != f and sy.known[f][e] < I[idx].val:
                    sy.known[f][e] = I[idx].val
                    waits.append(sy.sem_for(e, I[idx].val))
            for q in ("sp", "act", "pool"):
                n = sy.dcnt[q]
                for s in range(KRING):
                    uses = (n - s + KRING - 1) // KRING if n > s else 0
                    if uses > 0:
                        key = (q, id(sy.ring[q][s]))
                        if sy.knownd[f].get(key, 0) < 16 * uses:
                            sy.knownd[f][key] = 16 * uses
                            waits.append((sy.ring[q][s], 16 * uses))
            bar[f] = waits
        streams = {f: [ins for ins in I if ins.eng == f] for f in ("pe", "act", "dve", "pool", "sp")}

        def run(e, f):
            for ins in streams[f]:
                for s, v in ins.waits:
                    e.wait_ge(s, v)
                if ins.fn is None:
                    continue
                h = ins.fn(e)
                if ins.dma:
                    h.then_inc(ins.dsem, 16)
                elif ins.inc:
                    s, _ = sy.sem_for(f, ins.val)
                    h.then_inc(s, 1)
            for s, v in bar[f]:
                e.wait_ge(s, v)

        block.tensor(lambda e: run(e, "pe"))
        block.scalar(lambda e: run(e, "act"))
        block.vector(lambda e: run(e, "dve"))
        block.gpsimd(lambda e: run(e, "pool"))
        block.sync(lambda e: run(e, "sp"))


def _pp(v, nchunk):
    return np.ascontiguousarray(np.asarray(v, np.float32).reshape(nchunk, 128).T)


def _rope_tables(S):
    t = np.arange(S)
    row = (t // 64).astype(np.float32)
    col = (t % 64).astype(np.float32)
    inv = (np.float32(10000.0) ** (-np.arange(8, dtype=np.float32) / np.float32(8))).astype(np.float32)
    ar = (row[None, :] * inv[:, None]).astype(np.float32)
    ac = (col[None, :] * inv[:, None]).astype(np.float32)
    cr, sr, cc, sc = np.cos(ar), np.sin(ar), np.cos(ac), np.sin(ac)
    C = np.concatenate([cr, cr, cc, cc], 0).astype(np.float32)
    Sg = np.concatenate([-sr, sr, -sc, sc], 0).astype(np.float32)
    return np.ascontiguousarray(np.tile(C, (4, 1))), np.ascontiguousarray(np.tile(Sg, (4, 1)))


_PERM32 = np.concatenate([np.arange(8, 16), np.arange(0, 8), np.arange(24, 32), np.arange(16, 24)])


def host_prep(inp, S):
    f32 = np.float32
    g = lambda k: np.asarray(inp[k], f32)
    B = inp["x"].shape[0]
    w_in = g("w_in")[0]
    w_in_ext = np.ascontiguousarray(np.concatenate([w_in, w_in[:, OFF_KR + _PERM32]], axis=1))
    w_uq = g("mla_w_uq")[0].reshape(256, NH, 96)
    w_uq_n = np.ascontiguousarray(w_uq[:, :, :64].reshape(256, 512))
    w_uq_r = np.ascontiguousarray(w_uq[:, :, 64:].reshape(256, 256))
    w_uq_rb = np.ascontiguousarray(w_uq[:, :, 64 + _PERM32].reshape(256, 256))
    w_ukv = g("mla_w_ukv")[0].reshape(128, NH, 128)
    w_ukv_k = np.ascontiguousarray(w_ukv[:, :, :64].reshape(128, 512))
    w_ukv_v = np.ascontiguousarray(w_ukv[:, :, 64:].reshape(128, 512))
    bd = np.zeros((16, 128, 128), f32)
    for d in range(2):
        for gi, nm in enumerate(("lru_w_a", "lru_w_x")):
            w = g(nm)[0, d]
            for c in range(4):
                i = (d * 2 + gi) * 4 + c
                bd[i, :64, :64] = w[2 * c]
                bd[i, 64:, 64:] = w[2 * c + 1]
    lvec = np.zeros((128, 24), f32)
    for vi, nm in enumerate(("lru_b_a", "lru_b_x", "lru_lambda")):
        for d in range(2):
            lvec[:, vi * 8 + d * 4: vi * 8 + d * 4 + 4] = _pp(g(nm)[0, d], 4)
    lcw = np.concatenate([_pp(g("lru_conv_w")[0, k], 4) for k in range(4)], axis=1)
    fcw = np.concatenate([_pp(g("ffn_conv_w")[0, k], 44) for k in range(3)], axis=1)
    ropeC, ropeS = _rope_tables(S)
    common = {
        "w_mod": g("w_mod")[0], "bmod_pp": _pp(g("b_mod")[0], 48), "b_mod": g("b_mod"),
        "gvec_pp": np.concatenate([_pp(g("g_pre_mix")[0], 8), _pp(g("g_pre_ffn")[0], 8)], axis=1),
        "g_post": np.ascontiguousarray(np.stack([g("g_post_mix")[0], g("g_post_ffn")[0]])),
        "w_in_ext": w_in_ext, "lru_cw": np.ascontiguousarray(lcw), "lru_cb": _pp(g("lru_conv_b")[0], 4),
        "lru_bd": bd, "lru_vec": lvec,
        "gq": _pp(g("mla_g_q")[0], 2), "gkv": _pp(g("mla_g_kv")[0], 1),
        "w_uq_n": w_uq_n, "w_uq_r": w_uq_r, "w_uq_rb": w_uq_rb, "w_ukv_k": w_ukv_k, "w_ukv_v": w_ukv_v,
        "w_out": g("w_out")[0], "w_up": g("ffn_w_up")[0], "w_dn": g("ffn_w_down")[0],
        "ffn_cw": np.ascontiguousarray(fcw), "ffn_cb": _pp(g("ffn_conv_b")[0], 44),
        "ropeC": ropeC, "ropeS": ropeS, "ident": np.eye(128).astype(ml_dtypes.bfloat16),
    }
    maps = []
    cc = _pp(g("c_ctx"), 8)
    for b in range(B):
        m = dict(common)
        m["x"] = np.ascontiguousarray(g("x")[b, :S])
        m["ctx"] = np.ascontiguousarray(g("ctx")[b])
        m["cvec"] = np.ascontiguousarray(np.concatenate([_pp(g("c")[b], 8), cc], axis=1))
        maps.append(m)
    return maps


def build(S, dbg=(), stop=None):
    nc = bass.Bass("TRN2", target_bir_lowering=False)
    NK = CT + S
    NKC = NK // 128
    NT = S // 512
    XO_C, XO_L = 2, 262
    LX = 264 + S

    def din(name, shape, dt=F32):
        return nc.dram_tensor(name, list(shape), dt, kind="ExternalInput").ap()

    def dscr(name, shape, dt):
        kind = "ExternalOutput" if name in dbg else "Internal"
        return nc.dram_tensor(name, list(shape), dt, kind=kind).ap()

    x_d = din("x", [S, D]); ctx_d = din("ctx", [CT, D])
    cvec_d = din("cvec", [128, 16]); wmod_d = din("w_mod", [D, 6 * D])
    bmodpp_d = din("bmod_pp", [128, 48]); bmod_d = din("b_mod", [1, 6 * D])
    gvec_d = din("gvec_pp", [128, 16]); gpost_d = din("g_post", [2, D])
    win_d = din("w_in_ext", [D, 1472])
    lcw_d = din("lru_cw", [128, 16]); lcb_d = din("lru_cb", [128, 4])
    lbd_d = din("lru_bd", [16, 128, 128]); lvec_d = din("lru_vec", [128, 24])
    gq_d = din("gq", [128, 2]); gkv_d = din("gkv", [128, 1])
    wuqn_d = din("w_uq_n", [256, 512]); wuqr_d = din("w_uq_r", [256, 256]); wuqb_d = din("w_uq_rb", [256, 256])
    wukvk_d = din("w_ukv_k", [128, 512]); wukvv_d = din("w_ukv_v", [128, 512])
    wout_d = din("w_out", [D, D]); wup_d = din("w_up", [D, 2 * DFF]); wdn_d = din("w_dn", [DFF, D])
    fcw_d = din("ffn_cw", [128, 3 * 44]); fcb_d = din("ffn_cb", [128, 44])
    ropeC_d = din("ropeC", [128, S]); ropeS_d = din("ropeS", [128, S])
    ident_d = din("ident", [128, 128], BF16)
    out_d = nc.dram_tensor("out", [S, D], F32, kind="ExternalOutput").ap()

    qT_d = dscr("qT", [NH, 96, S], BF16)
    kT_d = dscr("kT", [NH, 64, NK], BF16)
    krT_d = dscr("krT", [32, NK], BF16)
    vimg_d = dscr("vimg", [NH, 128, NKC, 64], BF16)
    gT_d = dscr("gT", [4, 128, S], BF16)
    yT_d = dscr("yT", [8, 128, S], BF16)
    hf_d = dscr("hfs", [4, 128, S], BF16)
    x1_d = dscr("x1s", [S, D], F32)
    h2_d = dscr("h2s", [8, 128, S + 2], BF16)
    dbgmod_d = dscr("dbgmod", [128, 48 + 2048], F32) if "dbgmod" in dbg else None
    dbgxr_d = dscr("dbgxr", [4, 128, LX], BF16) if "dbgxr" in dbg else None

    with ExitStack() as es:
        sy = Sync(nc, es)

        def sb(st, name, shape, dt):
            return st.enter_context(nc.sbuf_tensor("s_" + name, list(shape), dt))

        def ps(st, name, shape, dt=F32):
            return st.enter_context(nc.psum_tensor("p_" + name, list(shape), dt))

        ident = sb(es, "ident", [128, 128], BF16)
        modv = sb(es, "modv", [128, 6, 8], F32)
        gpbc = sb(es, "gpbc", [128, 2, D], F32)
        onesb = sb(es, "onesb", [128, 128], BF16)
        onesf = sb(es, "onesf", [128, 128], F32)
        epsc = sb(es, "epsc", [128, 1], F32)

        XO_C, XO_L = 2, 262
        stX = ExitStack()
        xrT = sb(stX, "xrT", [128, 4, LX], BF16)
        stW = ExitStack()
        win = sb(stW, "win", [128, 8, 1472], BF16)
        wuq = sb(stW, "wuq", [128, 2, 1024], BF16)
        wukv = sb(stW, "wukv", [128, 1024], BF16)
        gqs = sb(stW, "gqs", [128, 2], F32)
        gkvs = sb(stW, "gkvs", [128, 1], F32)

        with ExitStack() as st, nc.Block() as block:
            P = Prog(sy)
            P.dma(win[:], win_d.rearrange("(k p) f -> p k f", p=128), (), ["win"], q="pool")
            P.dma(wuq[:, :, 0:512], wuqn_d.rearrange("(k p) f -> p k f", p=128), (), ["wuq"], q="pool")
            P.dma(wuq[:, :, 512:768], wuqr_d.rearrange("(k p) f -> p k f", p=128), (), ["wuq"], q="pool")
            P.dma(wuq[:, :, 768:1024], wuqb_d.rearrange("(k p) f -> p k f", p=128), (), ["wuq"], q="pool")
            P.dma(wukv[:, 0:512], wukvk_d, (), ["wukv"], q="pool")
            P.dma(wukv[:, 512:1024], wukvv_d, (), ["wukv"], q="pool")
            P.dma(gqs[:], gq_d, (), ["gqs"])
            P.dma(gkvs[:], gkv_d, (), ["gkvs"])
            cv = sb(st, "cv", [128, 16], F32)
            sv = sb(st, "sv", [128, 16], F32)
            th = sb(st, "th", [128, 16], F32)
            svb = sb(st, "svb", [128, 8, 128], F32)
            wm = sb(st, "wm", [128, 2, 8, 512], F32)
            bpp = sb(st, "bpp", [128, 48], F32)
            gv = sb(st, "gv", [128, 16], F32)
            modpp = sb(st, "modpp", [128, 4, 8, 2], F32)
            bbc = sb(st, "bbc", [128, 2, D], F32)
            gbc = sb(st, "gbc", [128, 2, D], F32)
            tmpb = sb(st, "tmpb", [128, 512], F32)
            pspp = ps(st, "pspp", [128, 4, 8, 2], F32)
            psbc = ps(st, "psbc", [128, 2, 512], F32)

            P.dma(ident[:], ident_d, (), ["ident"])
            P.dma(cv[:], cvec_d, (), ["cv"])
            P.dma(bpp[:], bmodpp_d, (), ["bpp"])
            P.dma(gv[:], gvec_d, (), ["gv"])
            P.dma(bbc[:, 0, :], bmod_d[0:1, 2 * D:3 * D].partition_broadcast(128), (), ["bbc0"])
            P.dma(bbc[:, 1, :], bmod_d[0:1, 5 * D:6 * D].partition_broadcast(128), (), ["bbc1"])
            P.dma(gbc[:, 0, :], gpost_d[0:1, :].partition_broadcast(128), (), ["gbc0"])
            P.dma(gbc[:, 1, :], gpost_d[1:2, :].partition_broadcast(128), (), ["gbc1"])
            P.memset(onesb[:], 1.0, ["onesb"])
            P.memset(onesf[:], 1.0, ["onesf"])
            P.memset(epsc[:], EPS, ["epsc"])
            P.act(th[:], cv[:], AF.Tanh, ["cv"], ["th"], scale=0.5)
            P.stt(sv[:], th[:], 1.0, cv[:], ALU.add, ALU.mult, ["th", "cv"], ["sv"])
            P.ts(sv[:], sv[:], 0.5, None, ALU.mult, None, ["sv"], ["sv"])
            for k in range(8):
                P.ts(svb[:, k, :], onesf[:], sv[:, k:k + 1], None, ALU.mult, None, ["onesf", "sv"], [f"svb{k}"])
            wmv = wmod_d.rearrange("(k p) f -> p k f", p=128)
            vec_of = {0: 0, 1: 1, 3: 2, 4: 3}
            for j in range(12):
                sl = j % 2
                P.dma(wm[:, sl, :, :], wmv[:, :, j * 512:(j + 1) * 512], (), [f"wm{sl}"], q="sp" if j % 2 == 0 else "act")
                v, half = j // 2, j % 2
                if v in vec_of:
                    vi = vec_of[v]
                    for fc in range(4):
                        ch = half * 4 + fc
                        for k in range(8):
                            P.mm(pspp[:, vi, ch, :], wm[:, sl, k, fc * 128:(fc + 1) * 128], sv[:, k:16:8],
                                 k == 0, k == 7, [f"wm{sl}", "sv"], ["pspp"])
                else:
                    vv = 0 if v == 2 else 1
                    for k in range(8):
                        P.mm(psbc[:, half, :], svb[:, k, :], wm[:, sl, k, :], k == 0, k == 7,
                             [f"wm{sl}", f"svb{k}"], [f"psbc{half}"])
                    P.tt(tmpb[:], psbc[:, half, :], bbc[:, vv, half * 512:(half + 1) * 512], ALU.add,
                         [f"psbc{half}", f"bbc{vv}"], ["tmpb"])
                    P.tt(gpbc[:, vv, half * 512:(half + 1) * 512], tmpb[:], gbc[:, vv, half * 512:(half + 1) * 512],
                         ALU.mult, ["tmpb", f"gbc{vv}"], [f"gpbc{vv}{half}"])
            for vi, vsrc in enumerate((0, 1, 3, 4)):
                for jj in range(2):
                    P.tt(modpp[:, vi, :, jj], pspp[:, vi, :, jj], bpp[:, vsrc * 8:(vsrc + 1) * 8], ALU.add,
                         ["pspp", "bpp"], [f"modpp{vi}{jj}"])
            P.stt(modv[:, 0, :], modpp[:, 1, :, 0], 1.0, gv[:, 0:8], ALU.add, ALU.mult, ["modpp10", "gv"], ["modv0"])
            P.cp(modv[:, 1, :], modpp[:, 0, :, 0], ["modpp00"], ["modv1"])
            P.stt(modv[:, 2, :], modpp[:, 1, :, 1], 1.0, gv[:, 0:8], ALU.add, ALU.mult, ["modpp11", "gv"], ["modv2"])
            P.cp(modv[:, 3, :], modpp[:, 0, :, 1], ["modpp01"], ["modv3"])
            P.stt(modv[:, 4, :], modpp[:, 3, :, 0], 1.0, gv[:, 8:16], ALU.add, ALU.mult, ["modpp30", "gv"], ["modv4"])
            P.cp(modv[:, 5, :], modpp[:, 2, :, 0], ["modpp20"], ["modv5"])
            if dbgmod_d is not None:
                P.dma(dbgmod_d[:, 0:48], modv[:].rearrange("p a b -> p (a b)"),
                      [f"modv{i}" for i in range(6)], ["dbgmod"])
                P.dma(dbgmod_d[:, 48:48 + 2048], gpbc[:].rearrange("p a b -> p (a b)"),
                      ["gpbc00", "gpbc01", "gpbc10", "gpbc11"], ["dbgmod2"])
            P.finalize(block)
        if stop == "P0":
            return nc

        tiles = [("c", 0, CT)] + [("l", i * 512, 512) for i in range(NT)]

        with ExitStack() as stAB:

            with ExitStack() as st, nc.Block() as block:
                P = Prog(sy)
                xt = sb(st, "xt", [128, 4, D], F32)
                xjunk = sb(st, "xjunk", [128, D], BF16)
                xh = sb(st, "xh", [128, 4, D], BF16)
                st8 = sb(st, "st8", [128, 3, 4], F32)
                hT = sb(st, "hT", [128, 8, 512], BF16)
                tabC2 = sb(st, "tabC", [128, 2, 512], F32)
                tabS2 = sb(st, "tabS", [128, 2, 512], F32)
                gx = sb(st, "gx", [128, 3, 512], F32)
                gst = sb(st, "gst", [128, 4, 512], BF16)
                sqq = sb(st, "sqq", [128, 3, 512], BF16)
                cqT = sb(st, "cqT", [128, 2, 512], BF16)
                ckvg = sb(st, "ckvg", [128, 512], F32)
                ckvn = sb(st, "ckvn", [128, 512], BF16)
                rq = sb(st, "rq", [128, 2, 512], F32)
                rkv = sb(st, "rkv", [128, 2, 512], F32)
                cqs = sb(st, "cqs", [128, 2, 512], F32)
                tr1 = sb(st, "tr1", [128, 2, 512], F32)
                qn_st = sb(st, "qn_st", [128, 4, 512], BF16)
                qr_st = sb(st, "qr_st", [128, 2, 512], BF16)
                kn_st = sb(st, "kn_st", [128, 4, 512], BF16)
                kr_st = sb(st, "kr_st", [32, 512], BF16)
                v_st = sb(st, "v_st", [128, 4, 512], BF16)
                psT = ps(st, "psT", [128, 2, 2, 512], BF16)
                psI = ps(st, "psI", [128, 3, 512], F32)
                psM = ps(st, "psM", [128, 3, 512], F32)
                cI = [0]
                cM = [0]

                def nI():
                    cI[0] += 1
                    return (cI[0] - 1) % 3

                def nM():
                    cM[0] += 1
                    return (cM[0] - 1) % 3

                for c in range(4):
                    P.memset(xrT[:, c, 0:2], 0.0, ["xrpad"])
                    P.memset(xrT[:, c, 258:262], 0.0, ["xrpad"])
                    P.memset(xrT[:, c, 262 + S:264 + S], 0.0, ["xrpad"])

                def load_x(ti):
                    kind, t0, T = tiles[ti]
                    src = ctx_d if kind == "c" else x_d
                    for j in range(T // 128):
                        P.dma(xt[:, j, :], src[t0 + j * 128:t0 + (j + 1) * 128, :], (), [f"xt{j}"],
                              q="sp" if j % 2 == 0 else "act")
                    if kind == "l":
                        P.dma(tabC2[:, ti % 2, :], ropeC_d[:, t0:t0 + T], (), [f"tabC{ti % 2}"])
                        P.dma(tabS2[:, ti % 2, :], ropeS_d[:, t0:t0 + T], (), [f"tabS{ti % 2}"])

                load_x(0)
                for ti, (kind, t0, T) in enumerate(tiles):
                    lat = kind == "l"
                    ns = T // 128
                    g0 = t0 + (CT if lat else 0)
                    xo = (XO_L if lat else XO_C) + t0
                    mi = 0 if lat else 2
                    tabC, tabS = tabC2[:, ti % 2, :], tabS2[:, ti % 2, :]
                    tC, tS = f"tabC{ti % 2}", f"tabS{ti % 2}"
                    for j in range(ns):
                        P.act(xjunk[:], xt[:, j, :], AF.Square, [f"xt{j}"], ["xjunk", f"ss{j}"], accum=st8[:, 0, j:j + 1])
                    P.ts(st8[:, 1, 0:ns], st8[:, 0, 0:ns], 1.0 / D, EPS, ALU.mult, ALU.add,
                         [f"ss{j}" for j in range(ns)], ["ms"])
                    P.act(st8[:, 1, 0:ns], st8[:, 1, 0:ns], AF.Sqrt, ["ms"], ["ms"])
                    P.recip(st8[:, 2, 0:ns], st8[:, 1, 0:ns], ["ms"], ["rstd"])
                    for j in range(ns):
                        P.ts(xh[:, j, :], xt[:, j, :], st8[:, 2, j:j + 1], None, ALU.mult, None,
                             [f"xt{j}", "rstd"], [f"xh{j}"])
                    for cp in range(4):
                        sl = cp % 2
                        for j in range(ns):
                            for cc in range(2):
                                c = cp * 2 + cc
                                P.tr(psT[:, sl, cc, j * 128:(j + 1) * 128], xh[:, j, c * 128:(c + 1) * 128], ident[:],
                                     [f"xh{j}", "ident"], [f"psT{sl}"])
                        for cc in range(2):
                            c = cp * 2 + cc
                            P.act(hT[:, c, 0:T], psT[:, sl, cc, 0:T], AF.Identity, [f"psT{sl}", f"modv{mi}", f"modv{mi + 1}"],
                                  [f"hT{c}"], scale=modv[:, mi, c:c + 1], bias=modv[:, mi + 1, c:c + 1])
                    if ti + 1 < len(tiles):
                        load_x(ti + 1)
                    hTr = [f"hT{c}" for c in range(8)]

                    def inproj(col0, M):
                        s = nI()
                        for k in range(8):
                            P.mm(psI[0:M, s, 0:T], win[:, k, col0:col0 + M], hT[:, k, 0:T], k == 0, k == 7,
                                 ["win"] + hTr, [f"psI{s}"])
                        return s

                    for oc in range(4):
                        s = inproj(oc * 128, 128)
                        P.act(xrT[:, oc, xo:xo + T], psI[:, s, 0:T], AF.Copy, [f"psI{s}"], [f"xr{ti}"])
                    if lat:
                        for oc in range(4):
                            s = inproj(OFF_GATE + oc * 128, 128)
                            P.act(gx[:, 0, :], psI[:, s, :], AF.Square, [f"psI{s}"], ["gx0"])
                            P.ts(gx[:, 1, :], gx[:, 0, :], 0.044715, 1.0, ALU.mult, ALU.add, ["gx0"], ["gx1"])
                            P.tt(gx[:, 1, :], gx[:, 1, :], psI[:, s, :], ALU.mult, ["gx1", f"psI{s}"], ["gx1"])
                            P.act(gx[:, 2, :], gx[:, 1, :], AF.Tanh, ["gx1"], ["gx2"], scale=GELU_C)
                            P.stt(gst[:, oc, :], gx[:, 2, :], 1.0, psI[:, s, :], ALU.add, ALU.mult,
                                  ["gx2", f"psI{s}"], [f"gst{oc}"])
                            P.dma(gT_d[oc, :, t0:t0 + T], gst[:, oc, :], [f"gst{oc}"], [f"gT{ti}"])
                        for kq in range(2):
                            s = inproj(OFF_CQ + kq * 128, 128)
                            P.act(sqq[:, kq, :], psI[:, s, :], AF.Square, [f"psI{s}"], [f"sqq{kq}"])
                            P.ts(cqT[:, kq, :], psI[:, s, :], gqs[:, kq:kq + 1], None, ALU.mult, None,
                                 [f"psI{s}", "gqs"], [f"cqT{kq}"])
                    s = inproj(OFF_CKV, 128)
                    P.act(sqq[:, 2, 0:T], psI[:, s, 0:T], AF.Square, [f"psI{s}"], ["sqq2"])
                    P.ts(ckvg[:, 0:T], psI[:, s, 0:T], gkvs[:, 0:1], None, ALU.mult, None, [f"psI{s}", "gkvs"], ["ckvg"])
                    sA = inproj(OFF_KR, 32)
                    sB = inproj(OFF_KR + 32, 32) if lat else None
                    if lat:
                        m = nM()
                        for kq in range(2):
                            P.mm(psM[:, m, :], onesb[:], sqq[:, kq, :], kq == 0, kq == 1, ["onesb", f"sqq{kq}"], [f"psM{m}"])
                        P.ts(rq[:, 0, :], psM[:, m, :], 1.0 / 256, EPS, ALU.mult, ALU.add, [f"psM{m}"], ["rq0"])
                        P.act(rq[:, 0, :], rq[:, 0, :], AF.Sqrt, ["rq0"], ["rq0"])
                        P.recip(rq[:, 1, :], rq[:, 0, :], ["rq0"], ["rq1"])
                    m = nM()
                    P.mm(psM[:, m, 0:T], onesb[:], sqq[:, 2, 0:T], True, True, ["onesb", "sqq2"], [f"psM{m}"])
                    P.ts(rkv[:, 0, 0:T], psM[:, m, 0:T], 1.0 / 128, EPS, ALU.mult, ALU.add, [f"psM{m}"], ["rkv0"])
                    P.act(rkv[:, 0, 0:T], rkv[:, 0, 0:T], AF.Sqrt, ["rkv0"], ["rkv0"])
                    P.recip(rkv[:, 1, 0:T], rkv[:, 0, 0:T], ["rkv0"], ["rkv1"])
                    P.tt(ckvn[:, 0:T], ckvg[:, 0:T], rkv[:, 1, 0:T], ALU.mult, ["ckvg", "rkv1"], ["ckvn"])
                    if lat:
                        P.tt(tr1[0:32, 0, :], psI[0:32, sA, :], tabC[0:32, :], ALU.mult, [f"psI{sA}", tC], ["tr1a"])
                        P.tt(tr1[0:32, 1, :], psI[0:32, sB, :], tabS[0:32, :], ALU.mult, [f"psI{sB}", tS], ["tr1b"])
                        P.tt(kr_st[:, :], tr1[0:32, 0, :], tr1[0:32, 1, :], ALU.add, ["tr1a", "tr1b"], ["kr_st"], eng="pool")
                    else:
                        P.act(kr_st[:, 0:T], psI[0:32, sA, 0:T], AF.Copy, [f"psI{sA}"], ["kr_st"])
                    P.dma(krT_d[:, g0:g0 + T], kr_st[:, 0:T], ["kr_st"], [f"krT{ti}"])
                    for c in range(4):
                        m = nM()
                        P.mm(psM[:, m, 0:T], wukv[:, c * 128:(c + 1) * 128], ckvn[:, 0:T], True, True, ["wukv", "ckvn"], [f"psM{m}"])
                        P.act(kn_st[:, c, 0:T], psM[:, m, 0:T], AF.Copy, [f"psM{m}"], [f"kn_st{c}"])
                        for hh in range(2):
                            P.dma(kT_d[2 * c + hh, :, g0:g0 + T], kn_st[hh * 64:(hh + 1) * 64, c, 0:T], [f"kn_st{c}"],
                                  [f"kT{ti}"], q="sp" if hh == 0 else "act")
                    for j in range(ns):
                        m = nM()
                        P.mm(psM[:, m, :], ckvn[:, j * 128:(j + 1) * 128], wukv[:, 512:1024], True, True, ["wukv", "ckvn"], [f"psM{m}"])
                        P.act(v_st[:, j, :], psM[:, m, :], AF.Copy, [f"psM{m}"], [f"v_st{j}"])
                        P.dma(vimg_d[:, :, g0 // 128 + j, :].rearrange("h p d -> p h d"),
                              v_st[:, j, :].rearrange("p (h d) -> p h d", h=NH), [f"v_st{j}"], [f"vimg{ti}"])
                    if lat:
                        P.tt(cqs[:, 0, :], tabC, rq[:, 1, :], ALU.mult, [tC, "rq1"], ["cqs0"])
                        P.tt(cqs[:, 1, :], tabS, rq[:, 1, :], ALU.mult, [tS, "rq1"], ["cqs1"])
                        for c in range(4):
                            m = nM()
                            for kq in range(2):
                                P.mm(psM[:, m, :], wuq[:, kq, c * 128:(c + 1) * 128], cqT[:, kq, :], kq == 0, kq == 1,
                                     ["wuq", f"cqT{kq}"], [f"psM{m}"])
                            P.tt(qn_st[:, c, :], psM[:, m, :], rq[:, 1, :], ALU.mult, [f"psM{m}", "rq1"], [f"qn_st{c}"])
                            for hh in range(2):
                                P.dma(qT_d[2 * c + hh, 0:64, t0:t0 + T], qn_st[hh * 64:(hh + 1) * 64, c, :], [f"qn_st{c}"],
                                      [f"qT{ti}"], q="sp" if hh == 0 else "act")
                        for c in range(2):
                            mA = nM()
                            for kq in range(2):
                                P.mm(psM[:, mA, :], wuq[:, kq, 512 + c * 128:512 + (c + 1) * 128], cqT[:, kq, :], kq == 0, kq == 1,
                                     ["wuq", f"cqT{kq}"], [f"psM{mA}"])
                            mB = nM()
                            for kq in range(2):
                                P.mm(psM[:, mB, :], wuq[:, kq, 768 + c * 128:768 + (c + 1) * 128], cqT[:, kq, :], kq == 0, kq == 1,
                                     ["wuq", f"cqT{kq}"], [f"psM{mB}"])
                            P.tt(tr1[:, 0, :], psM[:, mA, :], cqs[:, 0, :], ALU.mult, [f"psM{mA}", "cqs0"], ["tr1a"])
                            P.tt(tr1[:, 1, :], psM[:, mB, :], cqs[:, 1, :], ALU.mult, [f"psM{mB}", "cqs1"], ["tr1b"])
                            P.tt(qr_st[:, c, :], tr1[:, 0, :], tr1[:, 1, :], ALU.add, ["tr1a", "tr1b"], [f"qr_st{c}"], eng="pool")
                            for hh in range(4):
                                P.dma(qT_d[4 * c + hh, 64:96, t0:t0 + T], qr_st[hh * 32:(hh + 1) * 32, c, :], [f"qr_st{c}"],
                                      [f"qT{ti}"], q="sp" if hh % 2 == 0 else "act")
                if dbgxr_d is not None:
                    for c in range(4):
                        P.dma(dbgxr_d[c], xrT[:, c, :], [f"xr{ti}" for ti in range(len(tiles))] + ["xrpad"], ["dbgxr"])
                P.finalize(block)
            if stop == "PA":
                return nc
            stW.close()

            with ExitStack() as st, nc.Block() as block:
                P = Prog(sy)
                P.act_ord = True
                wg = sb(st, "wg", [128, 16, 128], BF16)
                lv = sb(st, "lv", [128, 24], F32)
                lcw = sb(st, "lcw", [128, 16], F32)
                lcb = sb(st, "lcb", [128, 4], F32)
                lc = sb(st, "lc", [128, 4, 8], F32)
                spt = sb(st, "spt", [128, 8], F32)
                xc = sb(st, "xc", [128, 2, 4, 512], BF16)
                wk = sb(st, "wk", [128, 2, 4, 4, 512], F32)
                car = sb(st, "car", [128, 8], F32)
                hst = sb(st, "hst", [128, 4, 512], BF16)
                hfl = sb(st, "hfl", [128, 2, 4, 512], BF16)
                gl = sb(st, "gl", [128, 2, 4, 512], BF16)
                yst = sb(st, "yst", [128, 2, 4, 512], BF16)
                psG = ps(st, "psG", [128, 6, 512], F32)
                psC = ps(st, "psC", [128, 2, 512], F32)
                dg = sb(st, "dg", [128, 16, 128], BF16)

                P.dma(wg[:], lbd_d.rearrange("i p m -> p i m"), (), ["wg"], q="pool")
                P.dma(lv[:], lvec_d, (), ["lv"])
                P.dma(lcw[:], lcw_d, (), ["lcw"])
                P.dma(lcb[:], lcb_d, (), ["lcb"])
                P.act(spt[:], lv[:, 16:24], AF.Exp, ["lv"], ["spt"], scale=-1.0)
                P.act(spt[:], spt[:], AF.Ln, ["spt"], ["spt"], bias=1.0)
                P.ts(lc[:, 3, :], spt[:], -8.0, None, ALU.mult, None, ["spt"], ["lc3"])
                P.ts(lc[:, 2, :], spt[:], -4.0, None, ALU.mult, None, ["spt"], ["lc2"])
                P.ts(lc[:, 0, :], lv[:, 0:8], 0.5, None, ALU.mult, None, ["lv"], ["lc0"])
                P.ts(lc[:, 1, :], lv[:, 8:16], 0.5, None, ALU.mult, None, ["lv"], ["lc1"])
                lcr = ["lc0", "lc1", "lc2", "lc3"]
                ntl = len(tiles)
                for i16 in range(16):
                    P.ts(dg[:, i16, :], ident[:], lcw[:, i16:i16 + 1], None, ALU.mult, None, ["lcw"], ["dg"])
                gcnt = [0]
                ccnt = [0]

                def lru_pass(d):
                    rev = d == 1
                    order = [0] + (list(range(ntl - 1, 0, -1)) if rev else list(range(1, ntl)))
                    for oi, ti in enumerate(order):
                        kind, t0, T = tiles[ti]
                        lat = kind == "l"
                        xo = (XO_L if lat else XO_C) + t0
                        sl = oi % 2
                        if rev and lat:
                            P.dma(hfl[:, sl, :, :], hf_d[:, :, t0:t0 + T].rearrange("c p t -> p c t"), [f"hf{ti}"], [f"hfl{sl}"])
                            P.dma(gl[:, sl, :, :], gT_d[:, :, t0:t0 + T].rearrange("c p t -> p c t"), (), [f"gl{sl}"], q="act")
                        if not rev:
                            nb = [f"xr{ti}"]
                            if lat and ti - 1 >= 1:
                                nb.append(f"xr{ti - 1}")
                            if lat and ti + 1 < ntl:
                                nb.append(f"xr{ti + 1}")
                            for c in range(4):
                                cs = ccnt[0] % 2
                                ccnt[0] += 1
                                for k in range(4):
                                    P.mm(psC[:, cs, 0:T], dg[:, k * 4 + c, :], xrT[:, c, xo + k - 2:xo + k - 2 + T], k == 0, k == 3,
                                         nb + ["dg", "xrpad"], [f"psC{cs}"])
                                P.act(xc[:, sl, c, 0:T], psC[:, cs, 0:T], AF.Identity, [f"psC{cs}", "lcb"], [f"xc{sl}{c}"],
                                      bias=lcb[:, c:c + 1])
                            if oi >= 1:
                                pti = order[oi - 1]
                                pk, pt0, pT = tiles[pti]
                                pxo = (XO_L if pk == "l" else XO_C) + pt0
                                for c in range(4):
                                    P.cp(xrT[:, c, pxo:pxo + pT], xc[:, 1 - sl, c, 0:pT], [f"xc{1 - sl}{c}"], [f"xr{pti}"], eng="pool")
                            xsrc = [xc[:, sl, c, 0:T] for c in range(4)]
                            xtok = [f"xc{sl}{c}" for c in range(4)]
                        else:
                            xsrc = [xrT[:, c, xo:xo + T] for c in range(4)]
                            xtok = [f"xr{ti}"] * 4
                        col = d * 4
                        for c in range(4):
                            for gi in range(2):
                                gs_ = gcnt[0] % 6
                                gcnt[0] += 1
                                P.mm(psG[:, gs_, 0:T], wg[:, (d * 2 + gi) * 4 + c, :], xsrc[c], True, True,
                                     ["wg", xtok[c]], [f"psG{gs_}"])
                                P.act(wk[:, sl, c, gi, 0:T], psG[:, gs_, 0:T], AF.Tanh, [f"psG{gs_}"] + lcr, [f"wk{sl}{c}{gi}"],
                                      scale=0.5, bias=lc[:, gi, col + c:col + c + 1])
                        for c in range(4):
                            P.act(wk[:, sl, c, 2, 0:T], wk[:, sl, c, 0, 0:T], AF.Exp, [f"wk{sl}{c}0"] + lcr, [f"wk{sl}{c}2"],
                                  scale=lc[:, 2, col + c:col + c + 1], bias=lc[:, 2, col + c:col + c + 1])
                            P.tt(wk[:, sl, c, 3, 0:T], wk[:, sl, c, 2, 0:T], wk[:, sl, c, 2, 0:T], ALU.mult, [f"wk{sl}{c}2"], [f"wk{sl}{c}3"], eng="pool")
                        for c in range(4):
                            P.act(wk[:, sl, c, 3, 0:T], wk[:, sl, c, 3, 0:T], AF.Sqrt, [f"wk{sl}{c}3"], [f"wk{sl}{c}3"], scale=-1.0, bias=1.0)
                        for c in range(4):
                            u_ = wk[:, sl, c, 1, 0:T]
                            P.stt(u_, u_, 1.0, xsrc[c], ALU.add, ALU.mult, [f"wk{sl}{c}1", xtok[c]], [f"wk{sl}{c}1"])
                            P.stt(u_, u_, 0.5, wk[:, sl, c, 3, 0:T], ALU.mult, ALU.mult, [f"wk{sl}{c}1", f"wk{sl}{c}3"], [f"wk{sl}{c}1"])
                            h_ = wk[:, sl, c, 0, 0:T]
                            a_ = wk[:, sl, c, 2, 0:T]
                            init = 0.0 if oi == 0 else car[:, col + c:col + c + 1]
                            rd = [f"wk{sl}{c}1", f"wk{sl}{c}2"] + ([] if oi == 0 else [f"car{col + c}"])
                            if rev:
                                P.scan(h_[:, ::-1], a_[:, ::-1], u_[:, ::-1], init, rd, [f"wk{sl}{c}0"])
                                P.cp(car[:, col + c:col + c + 1], h_[:, 0:1], [f"wk{sl}{c}0"], [f"car{col + c}"], eng="pool")
                            else:
                                P.scan(h_, a_, u_, init, rd, [f"wk{sl}{c}0"])
                                P.cp(car[:, col + c:col + c + 1], h_[:, T - 1:T], [f"wk{sl}{c}0"], [f"car{col + c}"], eng="pool")
                            if lat and not rev:
                                P.cp(hst[:, c, :], h_, [f"wk{sl}{c}0"], [f"hst{c}"], eng="pool")
                            if lat and rev:
                                P.tt(h_, h_, hfl[:, sl, c, :], ALU.add, [f"wk{sl}{c}0", f"hfl{sl}"], [f"wk{sl}{c}0"])
                                P.stt(yst[:, sl, c, :], h_, 0.5, gl[:, sl, c, :], ALU.mult, ALU.mult, [f"wk{sl}{c}0", f"gl{sl}"], [f"yst{sl}{c}"])
                        if lat and not rev:
                            P.dma(hf_d[:, :, t0:t0 + T].rearrange("c p t -> p c t"), hst[:], [f"hst{c}" for c in range(4)], [f"hf{ti}"])
                        if lat and rev:
                            P.dma(yT_d[0:4, :, t0:t0 + T].rearrange("c p t -> p c t"), yst[:, sl, :, :],
                                  [f"yst{sl}{c}" for c in range(4)], [f"yT{ti}"])
                    if not rev:
                        pti = order[-1]
                        pk, pt0, pT = tiles[pti]
                        pxo = (XO_L if pk == "l" else XO_C) + pt0
                        sl = (len(order) - 1) % 2
                        for c in range(4):
                            P.cp(xrT[:, c, pxo:pxo + pT], xc[:, sl, c, 0:pT], [f"xc{sl}{c}"], [f"xr{pti}"], eng="pool")

                lru_pass(0)
                lru_pass(1)
                if dbgxr_d is not None:
                    for c in range(4):
                        P.dma(dbgxr_d[c], xrT[:, c, :], [f"xr{ti}" for ti in range(len(tiles))] + ["xrpad"], ["dbgxr"])
                P.finalize(block)
            if stop == "PB":
                return nc

        stX.close()

        with ExitStack() as st, nc.Block() as block:
            P = Prog(sy)
            NQT = S // 512
            kTs = sb(st, "kTs", [128, 2, NK], BF16)
            vs = sb(st, "vs", [128, 2, NKC, 65], BF16)
            qTs = sb(st, "qTs", [128, 2, S], BF16)
            pT = sb(st, "pT", [128, 3, 2, 512], BF16)
            lsb = sb(st, "lsb", [128, 512], F32)
            rl = sb(st, "rl", [64, 512], F32)
            ost = sb(st, "ost", [64, 2, 512], BF16)
            psS = ps(st, "psS", [128, 2, 2, 512], F32)
            psL = ps(st, "psL", [64, 2, 512], F32)
            psO = ps(st, "psO", [128, 2, 512], F32)
            for sl in range(2):
                P.memset(vs[:, sl, :, 64:65], 1.0, [f"vone{sl}"])

            def load_head(h):
                sl = h % 2
                P.dma(kTs[0:64, sl, :], kT_d[h], (), [f"kTs{sl}"])
                P.dma(kTs[64:96, sl, :], krT_d, (), [f"kTs{sl}"], q="act")
                P.dma(vs[:, sl, :, 0:64], vimg_d[h], (), [f"vs{sl}"])
                P.dma(qTs[0:96, sl, :], qT_d[h], (), [f"qTs{sl}"], q="act")

            pairs = [(h, qt, kp) for h in range(NH) for qt in range(NQT) for kp in range(NKC // 2)]
            npairs = len(pairs)
            PPH = NQT * (NKC // 2)

            def emit_qk(i):
                h, qt, kp = pairs[i]
                hs = h % 2
                for j in range(2):
                    kc = 2 * kp + j
                    P.mm(psS[:, i % 2, j, :], kTs[0:96, hs, kc * 128:(kc + 1) * 128], qTs[0:96, hs, qt * 512:(qt + 1) * 512],
                         True, True, [f"kTs{hs}", f"qTs{hs}"], [f"psS{i % 2}{j}"])

            def emit_exp(i):
                P.act(pT[:, i % 3, :, :], psS[:, i % 2, :, :], AF.Exp, [f"psS{i % 2}0", f"psS{i % 2}1"], [f"pT{i % 3}"], scale=MLA_SCALE)

            def emit_pv(i):
                h, qt, kp = pairs[i]
                hs = h % 2
                osl = (h * NQT + qt) % 2
                for j in range(2):
                    kc = 2 * kp + j
                    P.mm(psO[0:65, osl, :], vs[:, hs, kc, 0:65], pT[:, i % 3, j, :], kc == 0, kc == NKC - 1,
                         [f"vs{hs}", f"vone{hs}", f"pT{i % 3}"], [f"psO{osl}"])
                if kp == NKC // 2 - 1:
                    P.cp(lsb[64:65, :], psO[64:65, osl, :], [f"psO{osl}"], ["lsb"])
                    P.mm(psL[:, osl, :], onesf[64:65, 0:64], lsb[64:65, :], True, True, ["lsb", "onesf"], [f"psL{osl}"])
                    P.recip(rl[:, :], psL[:, osl, :], [f"psL{osl}"], ["rl"])
                    P.tt(ost[:, osl, :], psO[0:64, osl, :], rl[:, :], ALU.mult, [f"psO{osl}", "rl"], [f"ost{osl}"])
                    P.dma(yT_d[4 + h // 2, (h % 2) * 64:(h % 2) * 64 + 64, qt * 512:(qt + 1) * 512], ost[:, osl, :],
                          [f"ost{osl}"], [f"yTm{h}_{qt}"], q="sp" if osl == 0 else "act")
                if (i + 1) % PPH == 0 and h + 2 < NH:
                    load_head(h + 2)

            load_head(0)
            if NH > 1:
                load_head(1)
            emit_qk(0)
            if npairs > 1:
                emit_qk(1)
            for i in range(npairs):
                emit_exp(i)
                if i + 2 < npairs:
                    emit_qk(i + 2)
                emit_pv(i)
            P.finalize(block)
        if stop == "PC":
            return nc

        stW2 = ExitStack()
        wup = sb(stW2, "wup", [128, 8, 2 * DFF], BF16)

        with ExitStack() as st, nc.Block() as block:
            P = Prog(sy)
            wout = sb(st, "wout", [128, 8, D], BF16)
            ycT = sb(st, "ycT", [128, 2, 8, 512], BF16)
            xs = sb(st, "xs", [128, 3, D], F32)
            t1 = sb(st, "t1", [128, 3, D], F32)
            x1 = sb(st, "x1", [128, 3, D], F32)
            junk = sb(st, "junk", [128, D], BF16)
            xh1 = sb(st, "xh1", [128, 4, D], BF16)
            h2T = sb(st, "h2T", [128, 2, 8, 512], BF16)
            sts = sb(st, "sts", [128, 2, 3, 4], F32)
            zt = sb(st, "zt", [128, 8, 1], BF16)
            psY = ps(st, "psY", [128, 3, 2, 512], F32)
            psT2 = ps(st, "psT2", [128, 2, 2, 512], BF16)
            P.dma(wout[:], wout_d.rearrange("(k p) f -> p k f", p=128), (), ["wout"], q="pool")
            wupv = wup_d.rearrange("(k p) f -> p k f", p=128)
            for q4 in range(4):
                P.dma(wup[:, :, q4 * 1408:(q4 + 1) * 1408], wupv[:, :, q4 * 1408:(q4 + 1) * 1408], (), ["wup"], q="pool")
            P.memset(zt[:], 0.0, ["zt"])
            P.dma(h2_d[:, :, 0:1].rearrange("c p t -> p c t"), zt[:], ["zt"], ["h2pad0"], slow=True)
            P.dma(h2_d[:, :, S + 1:S + 2].rearrange("c p t -> p c t"), zt[:], ["zt"], ["h2pad1"], slow=True)
            nsub = 0
            for ti in range(NT):
                t0 = ti * 512
                ysl = ti % 2
                P.dma(ycT[:, ysl, :, :], yT_d[:, :, t0:t0 + 512].rearrange("c p t -> p c t"), (), [f"ycT{ysl}"])
                for j in range(4):
                    xsl = nsub % 3
                    psl = nsub % 3
                    nsub += 1
                    r0 = t0 + j * 128
                    P.dma(xs[:, xsl, :], x_d[r0:r0 + 128, :], (), [f"xs{xsl}"], q="act")
                    for n in range(2):
                        for k in range(8):
                            P.mm(psY[:, psl, n, :], ycT[:, ysl, k, j * 128:(j + 1) * 128], wout[:, k, n * 512:(n + 1) * 512],
                                 k == 0, k == 7, [f"ycT{ysl}", "wout"], [f"psY{psl}{n}"])
                    pr = [f"psY{psl}0", f"psY{psl}1"]
                    P.act(junk[:], psY[:, psl, :, :].rearrange("p a b -> p (a b)"), AF.Square, pr, ["junk", "ssy"], accum=sts[:, 0, 0, j:j + 1])
                    P.ts(sts[:, 0, 1, j:j + 1], sts[:, 0, 0, j:j + 1], 1.0 / D, EPS, ALU.mult, ALU.add, ["ssy"], ["msy"])
                    P.act(sts[:, 0, 1, j:j + 1], sts[:, 0, 1, j:j + 1], AF.Sqrt, ["msy"], ["msy"])
                    P.recip(sts[:, 0, 2, j:j + 1], sts[:, 0, 1, j:j + 1], ["msy"], ["rsy"])
                    P.stt(t1[:, psl, :], psY[:, psl, :, :].rearrange("p a b -> p (a b)"), sts[:, 0, 2, j:j + 1], gpbc[:, 0, :],
                          ALU.mult, ALU.mult, pr + ["rsy"], [f"t1{psl}"])
                    P.tt(x1[:, xsl, :], t1[:, psl, :], xs[:, xsl, :], ALU.add, [f"t1{psl}", f"xs{xsl}"], [f"x1{xsl}"], eng="pool")
                    P.dma(x1_d[r0:r0 + 128, :], x1[:, xsl, :], [f"x1{xsl}"], [f"x1d{ti}_{j}"])
                    P.act(junk[:], x1[:, xsl, :], AF.Square, [f"x1{xsl}"], ["junk", "ss1"], accum=sts[:, 1, 0, j:j + 1])
                    P.ts(sts[:, 1, 1, j:j + 1], sts[:, 1, 0, j:j + 1], 1.0 / D, EPS, ALU.mult, ALU.add, ["ss1"], ["ms1"])
                    P.act(sts[:, 1, 1, j:j + 1], sts[:, 1, 1, j:j + 1], AF.Sqrt, ["ms1"], ["ms1"])
                    P.recip(sts[:, 1, 2, j:j + 1], sts[:, 1, 1, j:j + 1], ["ms1"], ["rs1"])
                    P.ts(xh1[:, j, :], x1[:, xsl, :], sts[:, 1, 2, j:j + 1], None, ALU.mult, None, [f"x1{xsl}", "rs1"], [f"xh1{j}"])
                hsl = ti % 2
                for cp in range(4):
                    sl = cp % 2
                    for j in range(4):
                        for cc in range(2):
                            c = cp * 2 + cc
                            P.tr(psT2[:, sl, cc, j * 128:(j + 1) * 128], xh1[:, j, c * 128:(c + 1) * 128], ident[:],
                                 [f"xh1{j}"], [f"psT{sl}"])
                    for cc in range(2):
                        c = cp * 2 + cc
                        P.act(h2T[:, hsl, c, :], psT2[:, sl, cc, :], AF.Identity, [f"psT{sl}"], [f"h2T{hsl}"],
                              scale=modv[:, 4, c:c + 1], bias=modv[:, 5, c:c + 1])
                P.dma(h2_d[:, :, 1 + t0:1 + t0 + 512].rearrange("c p t -> p c t"), h2T[:, hsl, :, :], [f"h2T{hsl}"], [f"h2d{ti}"])
            P.finalize(block)
        if stop == "PD1":
            return nc

        with ExitStack() as st, nc.Block() as block:
            P = Prog(sy)
            wdn = sb(st, "wdn", [128, FC, D], BF16)
            fcw = sb(st, "fcw", [128, 3 * 44], F32)
            fcb = sb(st, "fcb", [128, 44], F32)
            h2w = sb(st, "h2w", [128, 8, 514], BF16)
            upr = sb(st, "upr", [128, 2, 514], F32)
            ac = sb(st, "ac", [128, 3, 2, 512], F32)
            actT = sb(st, "actT", [128, FC, 512], BF16)
            x1t = sb(st, "x1t", [128, D], F32)
            t2 = sb(st, "t2", [128, D], F32)
            st2 = sb(st, "st2", [128, 3, 4], F32)
            psU = ps(st, "psU", [128, 2, 512], F32)
            psH = ps(st, "psH", [128, 2, 512], F32)
            psF = ps(st, "psF", [128, 2, 2, 512], F32)
            P.dma(wdn[:], wdn_d.rearrange("(k p) f -> p k f", p=128), (), ["wdn"], q="pool")
            P.dma(fcw[:], fcw_d, (), ["fcw"])
            P.dma(fcb[:], fcb_d, (), ["fcb"])
            nu = 0
            nf = 0
            for ti in range(NT):
                t0 = ti * 512
                P.dma(h2w[:], h2_d[:, :, t0:t0 + 514].rearrange("c p t -> p c t"), (), ["h2w"])

                def gate(fc_, extra):
                    b_ = fc_ % 3
                    P.act(ac[:, b_, 1, :], ac[:, b_, 1, :], AF.Silu, [f"ac{b_}1"] + extra, [f"ac{b_}1"])
                    P.tt(actT[:, fc_, :], ac[:, b_, 1, :], ac[:, b_, 0, :], ALU.mult, [f"ac{b_}0", f"ac{b_}1"], [f"actT{fc_}"])

                for fc in range(FC):
                    ab = fc % 3
                    for part in range(2):
                        fi = part * FC + fc
                        us = nu % 2
                        hs_ = nu % 2
                        nu += 1
                        for k in range(8):
                            P.mm(psU[:, us, :], wup[:, k, fi * 128:(fi + 1) * 128], h2w[:, k, 1:513], k == 0, k == 7,
                                 ["wup", "h2w"], [f"psU{us}"])
                        for k in range(8):
                            P.mm(psH[:, hs_, 0:2], wup[:, k, fi * 128:(fi + 1) * 128], h2w[:, k, 0:514:513], k == 0, k == 7,
                                 ["wup", "h2w"], [f"psH{hs_}"])
                        P.act(ac[:, ab, part, :], psU[:, us, :], AF.Identity, [f"psU{us}", "fcw", "fcb"], [f"ac{ab}{part}"],
                              scale=fcw[:, 44 + fi:45 + fi], bias=fcb[:, fi:fi + 1])
                        P.act(upr[:, hs_, 1:513], psU[:, us, :], AF.Copy, [f"psU{us}"],
                              [f"upr{hs_}"] + ([f"evac{fc}"] if part == 1 else []))
                        P.cp(upr[:, hs_, 0:514:513], psH[:, hs_, 0:2], [f"psH{hs_}"], [f"upr{hs_}"])
                        P.stt(ac[:, ab, part, :], upr[:, hs_, 0:512], fcw[:, fi:fi + 1], ac[:, ab, part, :], ALU.mult, ALU.add,
                              [f"upr{hs_}", f"ac{ab}{part}", "fcw"], [f"ac{ab}{part}"])
                        P.stt(ac[:, ab, part, :], upr[:, hs_, 2:514], fcw[:, 88 + fi:89 + fi], ac[:, ab, part, :], ALU.mult, ALU.add,
                              [f"upr{hs_}", f"ac{ab}{part}", "fcw"], [f"ac{ab}{part}"])
                    if fc >= 1:
                        gate(fc - 1, [f"evac{fc}"])
                gate(FC - 1, [])
                aT = [f"actT{fc}" for fc in range(FC)]
                for j in range(4):
                    fs = nf % 2
                    nf += 1
                    r0 = t0 + j * 128
                    P.dma(x1t[:], x1_d[r0:r0 + 128, :], (), ["x1t"], q="act")
                    for n in range(2):
                        for k in range(FC):
                            P.mm(psF[:, fs, n, :], actT[:, k, j * 128:(j + 1) * 128], wdn[:, k, n * 512:(n + 1) * 512],
                                 k == 0, k == FC - 1, aT + ["wdn"], [f"psF{fs}{n}"])
                    pr = [f"psF{fs}0", f"psF{fs}1"]
                    pv = psF[:, fs, :, :].rearrange("p a b -> p (a b)")
                    P.act(t2[:], pv, AF.Square, pr, ["t2", "ssf"], accum=st2[:, 0, j:j + 1])
                    P.ts(st2[:, 1, j:j + 1], st2[:, 0, j:j + 1], 1.0 / D, EPS, ALU.mult, ALU.add, ["ssf"], ["msf"])
                    P.act(st2[:, 1, j:j + 1], st2[:, 1, j:j + 1], AF.Sqrt, ["msf"], ["msf"])
                    P.recip(st2[:, 2, j:j + 1], st2[:, 1, j:j + 1], ["msf"], ["rsf"])
                    P.stt(t2[:], pv, st2[:, 2, j:j + 1], gpbc[:, 1, :], ALU.mult, ALU.mult, pr + ["rsf"], ["t2"])
                    P.tt(t2[:], t2[:], x1t[:], ALU.add, ["t2", "x1t"], ["t2"], eng="pool")
                    P.dma(out_d[r0:r0 + 128, :], t2[:], ["t2"], [f"out{ti}_{j}"])
            P.finalize(block)
        stW2.close()
    return nc


def kernel(**inputs):
    inp = {k: np.asarray(v) for k, v in inputs.items()}
    B, S = inp["x"].shape[0], inp["x"].shape[1]
    maps = host_prep(inp, S)
    nc = build(S)
    res = run_bass_kernel_spmd(nc, maps, core_ids=list(range(B)))
    out = np.stack([np.asarray(r["out"], dtype=np.float32) for r in res.results], axis=0)
    return out
```
